# Optimizing a Trainium2 kernel written in Bass

```python
import functools
import jax, jax.numpy as jnp
from jax import lax
import numpy as np

D_MODEL = 2048
BATCH = 4
SEQ = 2048
DEPTH = 2
DEC_BATCH = 128
DEC_SEQ = 8
PAST_LEN = 16384
PAGE_SIZE = 128

N_MIXERS = 2
N_CONV_LAYERS = (DEPTH + 1) // 2
N_RWKV_LAYERS = DEPTH // 2
D_CONV = D_MODEL
CONV_WIDTH = 3
HEAD_SIZE = 64
N_HEADS = D_MODEL // HEAD_SIZE
D_DECAY_LORA = max(32, int(round(1.8 * D_MODEL ** 0.5 / 32)) * 32)
D_AAA_LORA = max(32, int(round(1.8 * D_MODEL ** 0.5 / 32)) * 32)
D_GATE_LORA = max(32, int(round(0.6 * D_MODEL ** 0.8 / 32)) * 32)
D_FF = ((8 * D_MODEL // 3 + 255) // 256) * 256
N_SUBLAYERS = 3
N_SHIFT_MIX = 6
HALF_STEP = 0.5
RMS_EPS = 1e-6
GN_EPS = 64e-5
NORM_EPS = 1e-12

kernel_name = 'hybrid_shortconv_rwkv7_macaron_adaln_step'


def rms_norm(x, g):
    xf = x.astype(jnp.float32)
    y = xf * lax.rsqrt(jnp.mean(xf * xf, axis=-1, keepdims=True) + RMS_EPS)
    return (y * g.astype(jnp.float32)).astype(x.dtype)


def modulated_pre(x, g, shift, scale):
    return rms_norm(x, g) * (1 + scale) + shift


def gated_post(x, y, g, gate, res_w):
    return x + res_w * gate * rms_norm(y, g)


def swiglu(h, w_in, w_out):
    gt, up = jnp.split(h @ w_in, 2, axis=-1)
    return (jax.nn.silu(gt) * up) @ w_out


def short_conv_mixer(h, conv_state, w_in, conv_w, w_out):
    t_len = h.shape[1]
    b_gate, c_gate, xin = jnp.split(h @ w_in, 3, axis=-1)
    u = c_gate * xin
    u_ext = jnp.concatenate([conv_state.astype(u.dtype), u], axis=1)
    z = sum(conv_w[j] * u_ext[:, j:j + t_len] for j in range(CONV_WIDTH))
    y = (b_gate * z) @ w_out
    return y, u_ext[:, -(CONV_WIDTH - 1):]


def rwkv7_mixer(h, shift_state, wkv_state, mix, w0, w1, w2, a0, a1, a2, g1, g2,
                k_k, k_a, r_k, w_r, w_k, w_v, w_o, ln_w, ln_b):
    nb, t_len, d = h.shape
    h_prev = jnp.concatenate([shift_state[:, None].astype(h.dtype), h[:, :-1]], axis=1)
    xx = h_prev - h
    xs = h[:, :, None, :] + xx[:, :, None, :] * mix
    xr, xw, xk, xv, xa, xg = (xs[:, :, i] for i in range(N_SHIFT_MIX))
    r = xr @ w_r
    k = xk @ w_k
    v = xv @ w_v
    w_log = -jax.nn.softplus(-(w0 + jnp.tanh(xw @ w1) @ w2)) - 0.5
    a = jax.nn.sigmoid(a0 + (xa @ a1) @ a2)
    g = jax.nn.sigmoid(xg @ g1) @ g2

    def heads(t):
        return t.reshape(nb, t_len, N_HEADS, HEAD_SIZE).astype(jnp.float32)

    r, k, v, a = heads(r), heads(k), heads(v), heads(a)
    decay = jnp.exp(-jnp.exp(heads(w_log)))
    kk = k * k_k.reshape(N_HEADS, HEAD_SIZE).astype(jnp.float32)
    kk = kk / jnp.maximum(jnp.linalg.norm(kk, axis=-1, keepdims=True), NORM_EPS)
    k = k * (1 + (a - 1) * k_a.reshape(N_HEADS, HEAD_SIZE).astype(jnp.float32))

    def step(s, inp):
        r_t, w_t, k_t, v_t, a_t, b_t = inp
        sa = jnp.einsum('bhij,bhj->bhi', s, a_t)
        s = s * w_t[:, :, None, :] + sa[..., None] * b_t[:, :, None, :] + v_t[..., None] * k_t[:, :, None, :]
        return s, jnp.einsum('bhij,bhj->bhi', s, r_t)

    seq_first = lambda t: jnp.moveaxis(t, 1, 0)
    s_fin, y = lax.scan(step, wkv_state.astype(jnp.float32),
                        (seq_first(r), seq_first(decay), seq_first(k), seq_first(v),
                         seq_first(-kk), seq_first(kk * a)))
    y = jnp.moveaxis(y, 0, 1)
    mu = jnp.mean(y, axis=-1, keepdims=True)
    var = jnp.mean(jnp.square(y - mu), axis=-1, keepdims=True)
    y = ((y - mu) * lax.rsqrt(var + GN_EPS)).reshape(nb, t_len, d)
    y = y * ln_w.astype(jnp.float32) + ln_b.astype(jnp.float32)
    bonus = jnp.sum(r * k * r_k.astype(jnp.float32), axis=-1, keepdims=True) * v
    y = y + bonus.reshape(nb, t_len, d)
    out = (y.astype(h.dtype) * g) @ w_o
    return out, h[:, -1], s_fin.astype(wkv_state.dtype)


def trunk(x, c, conv_states, shift_states, wkv_states, *, mod_w, mod_b, norm_pre, norm_post,
          ffn_w_in, ffn_w_out, conv_w_in, conv_w, conv_w_out, rw_mix, rw_w0, rw_w1, rw_w2,
          rw_a0, rw_a1, rw_a2, rw_g1, rw_g2, rw_kk, rw_ka, rw_rk, rw_wr, rw_wk, rw_wv, rw_wo,
          rw_lnw, rw_lnb):
    nb = x.shape[0]
    new_conv, new_shift, new_wkv = [], [], []
    for l in range(DEPTH):
        mod = (jax.nn.silu(c) @ mod_w[l] + mod_b[l]).reshape(nb, 1, N_SUBLAYERS, 3, D_MODEL)
        shift, scale, gate = mod[:, :, :, 0], mod[:, :, :, 1], mod[:, :, :, 2]
        h = modulated_pre(x, norm_pre[l, 0], shift[:, :, 0], scale[:, :, 0])
        x = gated_post(x, swiglu(h, ffn_w_in[l, 0], ffn_w_out[l, 0]), norm_post[l, 0], gate[:, :, 0], HALF_STEP)
        h = modulated_pre(x, norm_pre[l, 1], shift[:, :, 1], scale[:, :, 1])
        j = l // N_MIXERS
        if l % N_MIXERS == 0:
            y, cs = short_conv_mixer(h, conv_states[j], conv_w_in[j], conv_w[j], conv_w_out[j])
            new_conv.append(cs)
        else:
            y, ss, ws = rwkv7_mixer(h, shift_states[j], wkv_states[j], rw_mix[j], rw_w0[j], rw_w1[j],
                                    rw_w2[j], rw_a0[j], rw_a1[j], rw_a2[j], rw_g1[j], rw_g2[j],
                                    rw_kk[j], rw_ka[j], rw_rk[j], rw_wr[j], rw_wk[j], rw_wv[j],
                                    rw_wo[j], rw_lnw[j], rw_lnb[j])
            new_shift.append(ss)
            new_wkv.append(ws)
        x = gated_post(x, y, norm_post[l, 1], gate[:, :, 1], 1.0)
        h = modulated_pre(x, norm_pre[l, 2], shift[:, :, 2], scale[:, :, 2])
        x = gated_post(x, swiglu(h, ffn_w_in[l, 1], ffn_w_out[l, 1]), norm_post[l, 2], gate[:, :, 2], HALF_STEP)
    return x, jnp.stack(new_conv), jnp.stack(new_shift), jnp.stack(new_wkv)


def setup_inputs(seed: int = 0) -> dict:
    key = jax.random.key(seed)
    ks = iter(jax.random.split(key, 48))

    def nrm(shape, std):
        return jax.random.normal(next(ks), shape, jnp.float32) * std

    def uni(shape, lo, hi):
        return jax.random.uniform(next(ks), shape, jnp.float32, lo, hi)

    d = D_MODEL
    nc, nr = N_CONV_LAYERS, N_RWKV_LAYERS
    return {
        'x_prompt': nrm((BATCH, SEQ, d), 1.0),
        'x_sample': nrm((DEC_BATCH, DEC_SEQ, d), 1.0),
        'state_conv': nrm((nc, DEC_BATCH, CONV_WIDTH - 1, D_CONV), 1.0),
        'state_shift': nrm((nr, DEC_BATCH, d), 1.0),
        'state_wkv': nrm((nr, DEC_BATCH, N_HEADS, HEAD_SIZE, HEAD_SIZE), 0.5),
        'c_prompt': nrm((BATCH, d), 1.0),
        'c_sample': nrm((DEC_BATCH, d), 1.0),
        'mod_w': nrm((DEPTH, d, N_SUBLAYERS * 3 * d), 0.5 * d ** -0.5),
        'mod_b': nrm((DEPTH, N_SUBLAYERS * 3 * d), 0.02),
        'norm_pre': 1.0 + nrm((DEPTH, N_SUBLAYERS, d), 0.1),
        'norm_post': 1.0 + nrm((DEPTH, N_SUBLAYERS, d), 0.1),
        'ffn_w_in': nrm((DEPTH, 2, d, 2 * D_FF), d ** -0.5),
        'ffn_w_out': nrm((DEPTH, 2, D_FF, d), D_FF ** -0.5),
        'conv_w_in': nrm((nc, d, 3 * D_CONV), d ** -0.5),
        'conv_w': nrm((nc, CONV_WIDTH, D_CONV), CONV_WIDTH ** -0.5),
        'conv_w_out': nrm((nc, D_CONV, d), D_CONV ** -0.5),
        'rw_mix': uni((nr, N_SHIFT_MIX, d), 0.0, 1.0),
        'rw_w0': uni((nr, d), -6.0, 1.0),
        'rw_w1': nrm((nr, d, D_DECAY_LORA), d ** -0.5),
        'rw_w2': nrm((nr, D_DECAY_LORA, d), 0.1 * D_DECAY_LORA ** -0.5),
        'rw_a0': nrm((nr, d), 0.1),
        'rw_a1': nrm((nr, d, D_AAA_LORA), d ** -0.5),
        'rw_a2': nrm((nr, D_AAA_LORA, d), 0.1 * D_AAA_LORA ** -0.5),
        'rw_g1': nrm((nr, d, D_GATE_LORA), d ** -0.5),
        'rw_g2': nrm((nr, D_GATE_LORA, d), D_GATE_LORA ** -0.5),
        'rw_kk': 0.85 + nrm((nr, d), 0.1),
        'rw_ka': 1.0 + nrm((nr, d), 0.1),
        'rw_rk': nrm((nr, N_HEADS, HEAD_SIZE), 0.1),
        'rw_wr': nrm((nr, d, d), d ** -0.5),
        'rw_wk': nrm((nr, d, d), d ** -0.5),
        'rw_wv': nrm((nr, d, d), d ** -0.5),
        'rw_wo': nrm((nr, d, d), d ** -0.5),
        'rw_lnw': 1.0 + nrm((nr, d), 0.1),
        'rw_lnb': nrm((nr, d), 0.02),
    }


def reference(x_prompt, x_sample, state_conv, state_shift, state_wkv, c_prompt, c_sample,
              mod_w, mod_b, norm_pre, norm_post, ffn_w_in, ffn_w_out, conv_w_in, conv_w, conv_w_out,
              rw_mix, rw_w0, rw_w1, rw_w2, rw_a0, rw_a1, rw_a2, rw_g1, rw_g2, rw_kk, rw_ka, rw_rk,
              rw_wr, rw_wk, rw_wv, rw_wo, rw_lnw, rw_lnb):
    run = functools.partial(
        trunk, mod_w=mod_w, mod_b=mod_b, norm_pre=norm_pre, norm_post=norm_post,
        ffn_w_in=ffn_w_in, ffn_w_out=ffn_w_out, conv_w_in=conv_w_in, conv_w=conv_w,
        conv_w_out=conv_w_out, rw_mix=rw_mix, rw_w0=rw_w0, rw_w1=rw_w1, rw_w2=rw_w2,
        rw_a0=rw_a0, rw_a1=rw_a1, rw_a2=rw_a2, rw_g1=rw_g1, rw_g2=rw_g2, rw_kk=rw_kk,
        rw_ka=rw_ka, rw_rk=rw_rk, rw_wr=rw_wr, rw_wk=rw_wk, rw_wv=rw_wv, rw_wo=rw_wo,
        rw_lnw=rw_lnw, rw_lnb=rw_lnb)
    nb = x_prompt.shape[0]
    conv0 = jnp.zeros((N_CONV_LAYERS, nb, CONV_WIDTH - 1, D_CONV), state_conv.dtype)
    shift0 = jnp.zeros((N_RWKV_LAYERS, nb, D_MODEL), state_shift.dtype)
    wkv0 = jnp.zeros((N_RWKV_LAYERS, nb, N_HEADS, HEAD_SIZE, HEAD_SIZE), state_wkv.dtype)
    y_prompt, conv_p, shift_p, wkv_p = run(x_prompt, c_prompt, conv0, shift0, wkv0)
    y_sample, conv_s, shift_s, wkv_s = run(x_sample, c_sample, state_conv, state_shift, state_wkv)
    return (y_prompt, y_sample, conv_p, shift_p, wkv_p, conv_s, shift_s, wkv_s)
```

```python
import os
import numpy as np
import concourse.bass as bass
import concourse.mybir as mybir
from concourse.bass_utils import run_bass_kernel_spmd

F32, BF16 = mybir.dt.float32, mybir.dt.bfloat16
AF = mybir.ActivationFunctionType
ALU = mybir.AluOpType

D = 2048; KC = 16; DFF = 5632; FC = 44
NPASS = 4; TPR = 512; NSS = 4; TSM = NSS * 8; TP = TPR + TSM
NSEQ = 17
NCORES = int(os.environ.get('K_NCORES', '8'))
RMS_EPS = 1e-6; GN_EPS = 64e-5
NRING = 6
NDS = 24
V_NPRE = 0
V_NPOST = 6
V_CW = 12
V_MIX = 15
V_W0 = 21; V_A0 = 22; V_KK = 23; V_KA = 24; V_RK = 25; V_LNW = 26; V_LNB = 27; V_OMKA = 28
NVEC = 29
DBG_NSUB = int(os.environ.get("K_NSUB", "6"))
K_RW = int(os.environ.get("K_RW", "99"))


class Eng:
    def __init__(self, nc, name, h):
        self.h = h
        self.name = name
        self.sem = nc.alloc_semaphore("c_" + name)
        self.cnt = 0
        self.seen = {}
        self.dsems = [nc.alloc_semaphore("d_%s%d" % (name, i)) for i in range(NDS)] if name in ("pool", "sp") else []
        self.dvals = [0] * NDS
        self.di = 0


class T:
    __slots__ = ("ap", "w", "r")

    def __init__(self, ap):
        self.ap = ap
        self.w = None
        self.r = {}

    def __getitem__(self, k):
        return self.ap[k]


class K:
    def __init__(self):
        nc = bass.Bass("TRN2", target_bir_lowering=False)
        self.nc = nc
        self.pe = Eng(nc, "pe", nc.tensor)
        self.act = Eng(nc, "act", nc.scalar)
        self.dve = Eng(nc, "dve", nc.vector)
        self.pool = Eng(nc, "pool", nc.gpsimd)
        self.sp = Eng(nc, "sp", nc.sync)
        self.engs = [self.pe, self.act, self.dve, self.pool, self.sp]
        self.psb = [T(nc.alloc_psum_tensor("ps%d" % i, [128, 512], F32).ap()) for i in range(8)]
        self.psi = 0
        self.ringi = 0

    def _deps(self, e, rd, wr):
        deps = {}
        for t in rd:
            if t.w is not None:
                s, v = t.w
                if deps.get(s, 0) < v:
                    deps[s] = v
        for t in wr:
            if t.w is not None:
                s, v = t.w
                if deps.get(s, 0) < v:
                    deps[s] = v
            for s, v in t.r.items():
                if deps.get(s, 0) < v:
                    deps[s] = v
        for s, v in deps.items():
            if e.seen.get(s, 0) >= v:
                continue
            e.h.wait_ge(s, v)
            e.seen[s] = v

    def op(self, e, fn, rd=(), wr=()):
        self._deps(e, rd, wr)
        ins = fn()
        e.cnt += 1
        ins.then_inc(e.sem, 1)
        for t in rd:
            if t.r.get(e.sem, 0) < e.cnt:
                t.r[e.sem] = e.cnt
        for t in wr:
            t.w = (e.sem, e.cnt)
            t.r = {}

    def dma(self, q, out_ap, in_ap, rd=(), wr=()):
        slot = q.di % NDS
        q.di += 1
        sem = q.dsems[slot]
        prev = q.dvals[slot]
        if prev and q.seen.get(sem, 0) < prev:
            q.h.wait_ge(sem, prev)
            q.seen[sem] = prev
        self._deps(q, rd, wr)
        ins = q.h.dma_start(out=out_ap, in_=in_ap)
        q.dvals[slot] = prev + 16
        ins.then_inc(sem, 16)
        v = prev + 16
        for t in rd:
            t.r[sem] = v
        for t in wr:
            t.w = (sem, v)
            t.r = {}

    def barrier(self, final=False):
        for e in self.engs:
            for o in self.engs:
                if o is e:
                    continue
                if o.cnt > e.seen.get(o.sem, 0):
                    e.h.wait_ge(o.sem, o.cnt)
                    e.seen[o.sem] = o.cnt
            for q in (self.pool, self.sp):
                for i in range(NDS):
                    v = q.dvals[i]
                    if v and e.seen.get(q.dsems[i], 0) < v:
                        e.h.wait_ge(q.dsems[i], v)
                        e.seen[q.dsems[i]] = v

    def ps(self):
        t = self.psb[self.psi % 8]
        self.psi += 1
        return t

    def mm(self, out_t, out_ap, pairs, rd):
        def fn():
            n = len(pairs)
            ins = None
            for i, (l, r) in enumerate(pairs):
                ins = self.nc.tensor.matmul(out_ap, l, r, start=(i == 0), stop=(i == n - 1))
            return ins
        self.op(self.pe, fn, rd=rd, wr=[out_t])

    def tr(self, out_t, out_ap, in_ap, ident_ap, rd):
        self.op(self.pe, lambda: self.nc.tensor.transpose(out_ap, in_ap, ident_ap), rd=rd, wr=[out_t])

    def a(self, out_ap, in_ap, func, rd, wr, bias=None, scale=1.0):
        if bias is None:
            f = lambda: self.nc.scalar.activation(out=out_ap, in_=in_ap, func=func, scale=scale)
        else:
            f = lambda: self.nc.scalar.activation(out=out_ap, in_=in_ap, func=func, bias=bias, scale=scale)
        self.op(self.act, f, rd=rd, wr=wr)

    def tt(self, out_ap, a_ap, b_ap, opx, rd, wr):
        self.op(self.dve, lambda: self.nc.vector.tensor_tensor(out_ap, a_ap, b_ap, opx), rd=rd, wr=wr)

    def ts(self, out_ap, a_ap, s1, s2, op0, op1, rd, wr):
        if s2 is None:
            f = lambda: self.nc.vector.tensor_scalar(out_ap, a_ap, s1, None, op0)
        else:
            f = lambda: self.nc.vector.tensor_scalar(out_ap, a_ap, s1, s2, op0, op1)
        self.op(self.dve, f, rd=rd, wr=wr)

    def stt(self, out_ap, a_ap, sc, b_ap, op0, op1, rd, wr):
        self.op(self.dve, lambda: self.nc.vector.scalar_tensor_tensor(out_ap, a_ap, sc, b_ap, op0, op1),
                rd=rd, wr=wr)

    def rsqrt(self, out_t, out_ap, in_ap, scale, bias, rd):
        self.a(out_ap, in_ap, AF.Sqrt, rd=rd, wr=[out_t], bias=bias, scale=scale)
        self.op(self.dve, lambda: self.nc.vector.reciprocal(out_ap, out_ap), rd=[out_t], wr=[out_t])

    def cp(self, out_ap, in_ap, rd, wr):
        self.op(self.dve, lambda: self.nc.vector.tensor_copy(out_ap, in_ap), rd=rd, wr=wr)


def build():
    k = K()
    nc = k.nc
    di = lambda name, shape: nc.dram_tensor(name, shape, F32, kind="ExternalInput").ap()
    do = lambda name, shape: nc.dram_tensor(name, shape, F32, kind="ExternalOutput").ap()
    xp = di("xp", [NPASS, 128, KC, TPR]); xs = di("xs", [NPASS, 128, KC, TSM])
    cT = di("cT", [128, KC, NSEQ])
    sconv = di("sconv", [NPASS, 128, KC, NSS, 2]); sshift = di("sshift", [NPASS, 128, KC, NSS])
    swkv = di("swkv", [NPASS, KC, 128, NSS, 64])
    vec = di("vec", [128, NVEC, KC]); modb = di("modb", [128, 2, 144])
    cst = di("cst", [128, 6, 128])
    mod_w = di("mod_w", [2, D, 9 * D])
    ffn_w_in = di("ffn_w_in", [2, 2, D, 2 * DFF]); ffn_w_out = di("ffn_w_out", [2, 2, DFF, D])
    conv_w_in = di("conv_w_in", [1, D, 3 * D]); conv_w_out = di("conv_w_out", [1, D, D])
    rw_w1 = di("rw_w1", [1, D, 96]); rw_w2 = di("rw_w2", [1, 96, D])
    rw_a1 = di("rw_a1", [1, D, 96]); rw_a2 = di("rw_a2", [1, 96, D])
    rw_g1 = di("rw_g1", [1, D, 256]); rw_g2 = di("rw_g2", [1, 256, D])
    rw_wr = di("rw_wr", [1, D, D]); rw_wk = di("rw_wk", [1, D, D]); rw_wv = di("rw_wv", [1, D, D])
    rw_wo = di("rw_wo", [1, D, D])
    yp = do("yp", [NPASS, 128, KC, TPR]); ys = do("ys", [NPASS, 128, KC, TSM])
    o_convp = do("o_convp", [128, KC, 2]); o_shiftp = do("o_shiftp", [128, KC])
    o_wkvp = do("o_wkvp", [128, KC, 64])
    o_convs = do("o_convs", [NPASS, 128, KC, NSS, 2]); o_shifts = do("o_shifts", [NPASS, 128, KC, NSS])
    o_wkvs = do("o_wkvs", [NPASS, KC, 128, NSS, 64])

    sb = lambda name, shape, dt=F32: T(nc.alloc_sbuf_tensor(name, shape, dt).ap())
    XT = sb("XT", [128, KC, TP])
    MOD = sb("MOD", [128, 2, 144, NSEQ])
    VEC = sb("VEC", [128, NVEC, KC])
    CST = sb("CST", [128, 6, 128])
    IDB = sb("IDB", [128, 128], BF16); ONB = sb("ONB", [128, 128], BF16)
    UC = sb("UC", [128, KC, 2]); HC = sb("HC", [128, KC]); HS = sb("HS", [128, KC, 64])
    RING = [sb("ring%d" % i, [128, 2048], BF16) for i in range(NRING)]
    ARENA_B = 120 * 1024
    ARENA = nc.alloc_sbuf_tensor("arena", [128, ARENA_B // 4], F32).ap()

    class Carver:
        def __init__(self):
            self.off = 0

        def get(self, shape, dt=F32):
            n = int(np.prod(shape[1:]))
            nb = n * (4 if dt == F32 else 2)
            nb4 = (nb + 31) // 32 * 32
            assert self.off + nb4 <= ARENA_B, ("arena overflow", self.off, nb4)
            w0 = self.off // 4
            ap = ARENA[:shape[0], w0:w0 + nb4 // 4]
            if dt != F32:
                ap = ap.bitcast(dt)
            ap = ap[:, :n]
            if len(shape) == 3:
                ap = ap.rearrange("p (a b) -> p a b", a=shape[1])
            elif len(shape) == 4:
                ap = ap.rearrange("p (a b c) -> p a b c", a=shape[1], b=shape[2])
            self.off += nb4
            return T(ap)

    ident = CST[:, 0, :]; blk = CST[:, 1, :]
    vcol = lambda vi, kd: VEC[:, vi, kd:kd + 1]

    k.dma(k.sp, VEC[:], vec, wr=[VEC])
    k.dma(k.sp, CST[:], cst, wr=[CST])
    k.ts(VEC[:, V_OMKA, :], VEC[:, V_KA, :], -1.0, 1.0, ALU.mult, ALU.add, rd=[VEC], wr=[VEC])
    k.dma(k.pool, IDB[:], cst[:, 0, :], wr=[IDB])
    k.dma(k.pool, ONB[:], cst[:, 5, :], wr=[ONB])
    k.op(k.dve, lambda: nc.vector.memset(UC[:], 0.0), wr=[UC])
    k.op(k.dve, lambda: nc.vector.memset(HC[:], 0.0), wr=[HC])
    k.op(k.dve, lambda: nc.vector.memset(HS[:], 0.0), wr=[HS])

    tiles = [(0, TPR), (TPR, TSM)]

    def gemm(W, krows, groups, acts, evac, mix=None, ncols=128):
        nk = (krows + 127) // 128
        for gi, cols in enumerate(groups):
            per_col = []
            for c0 in cols:
                sl = []
                kk0 = 0
                while kk0 < nk:
                    nkk = min(16, nk - kk0)
                    rt = RING[k.ringi % NRING]; k.ringi += 1
                    rows = min(128, krows - kk0 * 128) if nk == 1 else 128
                    dst = rt.ap[:rows, :nkk * ncols].rearrange("p (a b) -> p a b", a=nkk)
                    if nk == 1:
                        src = W[0:rows, c0:c0 + ncols].rearrange("(a p) f -> p a f", a=1)
                    else:
                        src = W[kk0 * 128:(kk0 + nkk) * 128, c0:c0 + ncols].rearrange("(a p) f -> p a f", p=128)
                    k.dma(k.pool, dst, src, wr=[rt])
                    rs = None
                    if mix is not None:
                        rs = RING[k.ringi % NRING]; k.ringi += 1
                        dst2 = rs.ap[:rows, :nkk * ncols].rearrange("p (a b) -> p a b", a=nkk)
                        mv = VEC[:, mix, kk0:kk0 + nkk].unsqueeze(2).broadcast_to([128, nkk, ncols])
                        k.tt(dst2, dst, mv, ALU.mult, rd=[rt, VEC], wr=[rs])
                    sl.append((kk0, nkk, rows, rt, rs))
                    kk0 += nkk
                per_col.append(sl)
            for ti, (t0, n) in enumerate(tiles):
                outs = []
                for ci, c0 in enumerate(cols):
                    pst = k.ps()
                    pairs = []; rd = []
                    for (kk0, nkk, rows, rt, rs) in per_col[ci]:
                        for j in range(nkk):
                            pairs.append((rt.ap[:rows, j * ncols:(j + 1) * ncols], acts[0][1](kk0 + j, t0, n)))
                            if rs is not None:
                                pairs.append((rs.ap[:rows, j * ncols:(j + 1) * ncols], acts[1][1](kk0 + j, t0, n)))
                        rd.append(rt)
                        if rs is not None:
                            rd.append(rs)
                    rd += [a_[0] for a_ in acts]
                    k.mm(pst, pst.ap[:ncols, :n], pairs, rd)
                    outs.append((pst, pst.ap[:ncols, :n]))
                evac(gi, ti, outs)

    cv = Carver()
    CS = cv.get([128, KC, NSEQ]); SCB = cv.get([128, KC, NSEQ], BF16); SG = cv.get([128, KC, NSEQ])
    k.dma(k.sp, CS[:], cT, wr=[CS])
    k.a(SG[:], CS[:], AF.Sigmoid, rd=[CS], wr=[SG])
    k.tt(SCB[:], CS[:], SG[:], ALU.mult, rd=[CS, SG], wr=[SCB])
    MB = cv.get([128, 2, 144])
    k.dma(k.sp, MB[:], modb, wr=[MB])
    mod_tiles = [(0, NSEQ)]
    for l in range(2):
        def ev(gi, ti, outs, l=l):
            for ci, (pst, pa) in enumerate(outs):
                f = gi * 2 + ci
                k.a(MOD[:, l, f, :], pa, AF.Identity, rd=[pst, MB], wr=[MOD], bias=MB[:, l, f:f + 1])
        old_tiles = tiles
        tiles = mod_tiles
        gemm(mod_w[l], D, [[f * 128, (f + 1) * 128] for f in range(0, 144, 2)],
             [(SCB, lambda kd, t0, n: SCB[:, kd, :])], ev)
        tiles = old_tiles
        for s in range(3):
            sc = MOD[:, l, (s * 3 + 1) * 16:(s * 3 + 2) * 16, :]
            gt = MOD[:, l, (s * 3 + 2) * 16:(s * 3 + 3) * 16, :]
            gpre = VEC[:, V_NPRE + l * 3 + s, :].unsqueeze(2).broadcast_to([128, KC, NSEQ])
            gpost = VEC[:, V_NPOST + l * 3 + s, :].unsqueeze(2).broadcast_to([128, KC, NSEQ])
            k.stt(sc, sc, 1.0, gpre, ALU.add, ALU.mult, rd=[MOD, VEC], wr=[MOD])
            k.stt(gt, gt, (1.0 if s == 1 else 0.5), gpost, ALU.mult, ALU.mult, rd=[MOD, VEC], wr=[MOD])
    k.barrier()

    def modv(l, s, kind, kd, seq0, nseq):
        return MOD[:, l, (s * 3 + kind) * 16 + kd, seq0:seq0 + nseq]

    def sumsq_rstd(src, RSTD, SQ, eps, p):
        for (t0, n) in tiles:
            pst = k.ps()
            for kd in range(KC):
                sq = SQ[kd % 2]
                k.a(sq[:, :n], src[:, kd, t0:t0 + n], AF.Square, rd=[src], wr=[sq])
                k.op(k.pe, lambda sq=sq, kd=kd, pst=pst, n=n: nc.tensor.matmul(
                    pst.ap[:, :n], ONB[:], sq[:, :n], start=(kd == 0), stop=(kd == KC - 1)),
                    rd=[sq, ONB], wr=[pst])
            k.rsqrt(RSTD, RSTD[:, t0:t0 + n], pst.ap[:, :n], 1.0 / D, eps, rd=[pst])

    def prenorm(l, s, p, RSTD, SQ, TMP, sink):
        sumsq_rstd(XT, RSTD, SQ, RMS_EPS, p)
        for kd in range(KC):
            for ti, (t0, n) in enumerate(tiles):
                tmp = TMP[(kd * 2 + ti) % 2]
                if ti == 0:
                    k.stt(tmp[:, :n], XT[:, kd, t0:t0 + n], modv(l, s, 1, kd, 0, 1), RSTD[:, t0:t0 + n],
                          ALU.mult, ALU.mult, rd=[XT, MOD, RSTD], wr=[tmp])
                    k.a(tmp[:, :n], tmp[:, :n], AF.Identity, rd=[tmp, MOD], wr=[tmp], bias=modv(l, s, 0, kd, 0, 1))
                else:
                    q0 = 1 + NSS * p
                    t3 = tmp[:, :n].rearrange("p (a b) -> p a b", a=NSS)
                    k.tt(tmp[:, :n], XT[:, kd, t0:t0 + n], RSTD[:, t0:t0 + n], ALU.mult, rd=[XT, RSTD], wr=[tmp])
                    k.tt(t3, t3, modv(l, s, 1, kd, q0, NSS).unsqueeze(2).broadcast_to([128, NSS, 8]), ALU.mult,
                         rd=[tmp, MOD], wr=[tmp])
                    k.tt(t3, t3, modv(l, s, 0, kd, q0, NSS).unsqueeze(2).broadcast_to([128, NSS, 8]), ALU.add,
                         rd=[tmp, MOD], wr=[tmp])
                sink(kd, ti, t0, n, tmp)

    def postnorm(l, s, p, Y, RSTD, SQ, TMP):
        sumsq_rstd(Y, RSTD, SQ, RMS_EPS, p)
        for kd in range(KC):
            for ti, (t0, n) in enumerate(tiles):
                tmp = TMP[(kd * 2 + ti) % 2]
                if ti == 0:
                    k.stt(tmp[:, :n], Y[:, kd, t0:t0 + n], modv(l, s, 2, kd, 0, 1), RSTD[:, t0:t0 + n],
                          ALU.mult, ALU.mult, rd=[Y, MOD, RSTD], wr=[tmp])
                else:
                    q0 = 1 + NSS * p
                    t3 = tmp[:, :n].rearrange("p (a b) -> p a b", a=NSS)
                    k.tt(tmp[:, :n], Y[:, kd, t0:t0 + n], RSTD[:, t0:t0 + n], ALU.mult, rd=[Y, RSTD], wr=[tmp])
                    k.tt(t3, t3, modv(l, s, 2, kd, q0, NSS).unsqueeze(2).broadcast_to([128, NSS, 8]), ALU.mult,
                         rd=[tmp, MOD], wr=[tmp])
                k.tt(XT[:, kd, t0:t0 + n], XT[:, kd, t0:t0 + n], tmp[:, :n], ALU.add, rd=[XT, tmp], wr=[XT])

    def ffn(l, which, s, p):
        cv = Carver()
        YH = cv.get([128, KC, TP])
        Hh = T(YH.ap.rearrange("p a b -> p (a b)").bitcast(BF16)[:, :KC * TP].rearrange("p (a b) -> p a b", a=KC))
        AT = cv.get([128, FC, TP], BF16)
        RSTD = cv.get([128, TP]); SQ = [cv.get([128, 512], BF16) for _ in range(2)]
        TMP = [cv.get([128, 512]) for _ in range(2)]
        SGT = [cv.get([128, 512]) for _ in range(2)]

        def sink(kd, ti, t0, n, tmp):
            k.a(Hh[:, kd, t0:t0 + n], tmp[:, :n], AF.Identity, rd=[tmp], wr=[Hh])
        prenorm(l, s, p, RSTD, SQ, TMP, sink)

        cnt = [0]

        def ev_in(gi, ti, outs):
            t0, n = tiles[ti]
            (pg, ga), (pu, ua) = outs
            sg = SGT[cnt[0] % 2]; cnt[0] += 1
            k.a(sg[:, :n], ga, AF.Silu, rd=[pg], wr=[sg])
            k.tt(AT[:, gi, t0:t0 + n], sg[:, :n], ua, ALU.mult, rd=[sg, pu], wr=[AT])
        gemm(ffn_w_in[l, which], D, [[j * 128, DFF + j * 128] for j in range(FC)],
             [(Hh, lambda kd, t0, n: Hh[:, kd, t0:t0 + n])], ev_in)

        def ev_out(gi, ti, outs):
            t0, n = tiles[ti]
            k.a(YH[:, gi, t0:t0 + n], outs[0][1], AF.Identity, rd=[outs[0][0]], wr=[YH, Hh])
        gemm(ffn_w_out[l, which], DFF, [[i * 128] for i in range(KC)],
             [(AT, lambda kd, t0, n: AT[:, kd, t0:t0 + n])], ev_out)
        postnorm(l, s, p, YH, RSTD, SQ, TMP)
        k.barrier()

    def conv(p):
        l, s = 0, 1
        cv = Carver()
        Hh = cv.get([128, KC, TP], BF16); BZ = cv.get([128, KC, TP], BF16); Y = cv.get([128, KC, TP])
        RSTD = cv.get([128, TP]); SQ = [cv.get([128, 512], BF16) for _ in range(2)]
        TMP = [cv.get([128, 512]) for _ in range(2)]
        UT = cv.get([128, 2 + TPR]); CT = cv.get([128, 512]); ZT = cv.get([128, 512])
        US = cv.get([128, NSS, 10])
        SCV = cv.get([128, KC, NSS, 2]); CSO = cv.get([128, KC, NSS, 2])
        k.dma(k.sp, SCV[:], sconv[p], wr=[SCV])

        def sink(kd, ti, t0, n, tmp):
            k.a(Hh[:, kd, t0:t0 + n], tmp[:, :n], AF.Identity, rd=[tmp], wr=[Hh])
        prenorm(l, s, p, RSTD, SQ, TMP, sink)
        cw = lambda i, j: VEC[:, V_CW + j, i:i + 1]

        def ev(i, ti, outs):
            t0, n = tiles[ti]
            (pb, ba), (pc, ca), (px, xa) = outs
            k.a(CT[:, :n], ca, AF.Identity, rd=[pc], wr=[CT])
            if ti == 0:
                k.a(UT[:, 0:2], UC[:, i, :], AF.Identity, rd=[UC], wr=[UT])
                k.tt(UT[:, 2:2 + n], CT[:, :n], xa, ALU.mult, rd=[CT, px, UT], wr=[UT])
                k.ts(ZT[:, :n], UT[:, 0:n], cw(i, 0), None, ALU.mult, None, rd=[UT, VEC], wr=[ZT])
                k.stt(ZT[:, :n], UT[:, 1:n + 1], cw(i, 1), ZT[:, :n], ALU.mult, ALU.add, rd=[UT, VEC, ZT], wr=[ZT])
                k.stt(ZT[:, :n], UT[:, 2:n + 2], cw(i, 2), ZT[:, :n], ALU.mult, ALU.add, rd=[UT, VEC, ZT], wr=[ZT])
                k.tt(BZ[:, i, t0:t0 + n], ZT[:, :n], ba, ALU.mult, rd=[ZT, pb], wr=[BZ])
                k.a(UC[:, i, :], UT[:, n:n + 2], AF.Identity, rd=[UT], wr=[UC])
            else:
                v3 = lambda ap: ap.rearrange("p (a b) -> p a b", a=NSS)
                k.a(US[:, :, 0:2], SCV[:, i, :, :], AF.Identity, rd=[SCV], wr=[US])
                k.tt(US[:, :, 2:10], v3(CT[:, :n]), v3(xa), ALU.mult, rd=[CT, px, US], wr=[US])
                z3 = v3(ZT[:, :n])
                k.ts(z3, US[:, :, 0:8], cw(i, 0), None, ALU.mult, None, rd=[US, VEC], wr=[ZT])
                k.stt(z3, US[:, :, 1:9], cw(i, 1), z3, ALU.mult, ALU.add, rd=[US, VEC, ZT], wr=[ZT])
                k.stt(z3, US[:, :, 2:10], cw(i, 2), z3, ALU.mult, ALU.add, rd=[US, VEC, ZT], wr=[ZT])
                k.tt(BZ[:, i, t0:t0 + n], ZT[:, :n], ba, ALU.mult, rd=[ZT, pb], wr=[BZ])
                k.a(CSO[:, i, :, :], US[:, :, 8:10], AF.Identity, rd=[US], wr=[CSO])
        gemm(conv_w_in[0], D, [[i * 128, D + i * 128, 2 * D + i * 128] for i in range(KC)],
             [(Hh, lambda kd, t0, n: Hh[:, kd, t0:t0 + n])], ev)
        k.dma(k.sp, o_convs[p], CSO[:], rd=[CSO])
        if p == NPASS - 1:
            k.dma(k.sp, o_convp, UC[:], rd=[UC])

        def ev_out(gi, ti, outs):
            t0, n = tiles[ti]
            k.a(Y[:, gi, t0:t0 + n], outs[0][1], AF.Identity, rd=[outs[0][0]], wr=[Y])
        gemm(conv_w_out[0], D, [[i * 128] for i in range(KC)],
             [(BZ, lambda kd, t0, n: BZ[:, kd, t0:t0 + n])], ev_out)
        postnorm(l, s, p, Y, RSTD, SQ, TMP)
        k.barrier()

    def rwkv(p):
        l, s = 1, 1
        cv = Carver()
        HX = cv.get([128, 2 * KC, TP], BF16)
        Y = T(HX.ap.rearrange("p a b -> p (a b)").bitcast(F32)[:, :KC * TP].rearrange("p (a b) -> p a b", a=KC))
        YG = cv.get([128, KC, TP], BF16)
        LW = cv.get([128, TP], BF16); LA = cv.get([128, TP], BF16); LG = cv.get([128, 2, TP], BF16)
        RSTD = cv.get([128, TP]); SQ = [cv.get([128, 512], BF16) for _ in range(2)]
        TMP = [cv.get([128, 512]) for _ in range(2)]
        HT = cv.get([128, 1 + TPR]); HSM = cv.get([128, NSS, 9])
        SSH = cv.get([128, KC, NSS]); SHO = cv.get([128, KC, NSS])
        SWB = [cv.get([128, NSS, 64]) for _ in range(2)]
        k.dma(k.sp, SSH[:], sshift[p], wr=[SSH])

        def sink(kd, ti, t0, n, tmp):
            if ti == 0:
                k.a(HT[:, 0:1], HC[:, kd:kd + 1], AF.Identity, rd=[HC], wr=[HT])
                k.a(HT[:, 1:1 + n], tmp[:, :n], AF.Identity, rd=[tmp, HT], wr=[HT])
                k.a(HX[:, kd, t0:t0 + n], tmp[:, :n], AF.Identity, rd=[tmp], wr=[HX])
                k.tt(HX[:, KC + kd, t0:t0 + n], HT[:, 0:n], HT[:, 1:n + 1], ALU.subtract, rd=[HT], wr=[HX])
                k.a(HC[:, kd:kd + 1], HT[:, n:n + 1], AF.Identity, rd=[HT], wr=[HC])
            else:
                v3 = lambda ap: ap.rearrange("p (a b) -> p a b", a=NSS)
                k.a(HSM[:, :, 0:1], SSH[:, kd, :].unsqueeze(2), AF.Identity, rd=[SSH], wr=[HSM])
                k.a(HSM[:, :, 1:9], v3(tmp[:, :n]), AF.Identity, rd=[tmp, HSM], wr=[HSM])
                k.a(HX[:, kd, t0:t0 + n], tmp[:, :n], AF.Identity, rd=[tmp], wr=[HX])
                k.tt(v3(HX[:, KC + kd, t0:t0 + n]), HSM[:, :, 0:8], HSM[:, :, 1:9], ALU.subtract, rd=[HSM], wr=[HX])
                k.a(SHO[:, kd, :].unsqueeze(2), HSM[:, :, 8:9], AF.Identity, rd=[HSM], wr=[SHO])
        prenorm(l, s, p, RSTD, SQ, TMP, sink)
        k.dma(k.sp, o_shifts[p], SHO[:], rd=[SHO])
        if p == NPASS - 1:
            k.dma(k.sp, o_shiftp, HC[:], rd=[HC])
        acts2 = [(HX, lambda kd, t0, n: HX[:, kd, t0:t0 + n]), (HX, lambda kd, t0, n: HX[:, KC + kd, t0:t0 + n])]

        def ev_w(gi, ti, outs):
            t0, n = tiles[ti]
            k.a(LW[:96, t0:t0 + n], outs[0][1], AF.Tanh, rd=[outs[0][0]], wr=[LW])
        gemm(rw_w1[0], D, [[0]], acts2, ev_w, mix=V_MIX + 1, ncols=96)

        def ev_a(gi, ti, outs):
            t0, n = tiles[ti]
            k.a(LA[:96, t0:t0 + n], outs[0][1], AF.Identity, rd=[outs[0][0]], wr=[LA])
        gemm(rw_a1[0], D, [[0]], acts2, ev_a, mix=V_MIX + 4, ncols=96)

        def ev_g(gi, ti, outs):
            t0, n = tiles[ti]
            k.a(LG[:, gi, t0:t0 + n], outs[0][1], AF.Sigmoid, rd=[outs[0][0]], wr=[LG])
        gemm(rw_g1[0], D, [[0], [128]], acts2, ev_g, mix=V_MIX + 5)

        Fn = lambda: cv.get([128, TP])
        R = Fn(); KR = Fn(); V = Fn(); WS = Fn(); A = Fn(); G = Fn()
        KKN = Fn(); KM = Fn(); BP = KR; BON = A; EP = Fn(); T1 = Fn(); T2 = Fn(); YF = R
        Bn = lambda w=1: cv.get([128, w * TP], BF16)
        ART = Bn(2); KT = Bn(); BT = Bn(); KPT = Bn(); BPT = Bn(); VB = Bn()
        NCH = 4
        TOK = [cv.get([64, 3, 128], BF16) for _ in range(NCH)]
        AM = [cv.get([64, 4, 128], BF16) for _ in range(NCH)]
        INV = [[cv.get([64, 2, 3, 64], BF16) for _ in range(2)] for _ in range(NCH)]
        WB = cv.get([64, 2, 64], BF16); UB = cv.get([64, 2, 64], BF16); YTK = cv.get([64, 128]); WT = cv.get([64, 2, 64])
        HB = cv.get([128, 64], BF16)
        MASKA = {64: CST[:64, 2, :], 8: CST[:8, 3, 0:16]}
        MASKL = {64: CST[:64, 4, 0:64], 8: CST[:8, 4, 64:72]}
        RM = {64: CST[:, 2, :], 8: CST[:, 3, :]}

        def store_ev(dst):
            def ev(gi, ti, outs, dst=dst):
                t0, n = tiles[ti]
                k.a(dst[:, t0:t0 + n], outs[0][1], AF.Identity, rd=[outs[0][0]], wr=[dst])
            return ev

        if K_RW < 11:
            k.op(k.dve, lambda: nc.vector.memset(YG[:], 0.0), wr=[YG])
        for i in range(KC if K_RW >= 2 else 0):
            SW = SWB[i % 2]
            k.dma(k.sp, SW[:], swkv[p, i], wr=[SW])
            gemm(rw_wr[0], D, [[i * 128]], acts2, store_ev(R), mix=V_MIX + 0)
            gemm(rw_wk[0], D, [[i * 128]], acts2, store_ev(KR), mix=V_MIX + 2)
            gemm(rw_wv[0], D, [[i * 128]], acts2, store_ev(V), mix=V_MIX + 3)

            def ev_ws(gi, ti, outs, i=i):
                t0, n = tiles[ti]
                k.a(WS[:, t0:t0 + n], outs[0][1], AF.Sigmoid, rd=[outs[0][0], VEC], wr=[WS], bias=vcol(V_W0, i))
            gemm(rw_w2[0], 96, [[i * 128]], [(LW, lambda kd, t0, n: LW[:96, t0:t0 + n])], ev_ws)

            def ev_as(gi, ti, outs, i=i):
                t0, n = tiles[ti]
                k.a(A[:, t0:t0 + n], outs[0][1], AF.Sigmoid, rd=[outs[0][0], VEC], wr=[A], bias=vcol(V_A0, i))
            gemm(rw_a2[0], 96, [[i * 128]], [(LA, lambda kd, t0, n: LA[:96, t0:t0 + n])], ev_as)
            gemm(rw_g2[0], 256, [[i * 128]], [(LG, lambda kd, t0, n: LG[:, kd, t0:t0 + n])], store_ev(G))

            for ti, (t0, n) in enumerate(tiles):
                C = 64 if ti == 0 else 8
                nch = n // C
                sl = slice(t0, t0 + n)
                v3 = lambda ap, C=C: ap.rearrange("p (a b) -> p a b", b=C)
                k.ts(T1[:, sl], KR[:, sl], vcol(V_KK, i), None, ALU.mult, None, rd=[KR, VEC], wr=[T1])
                k.tt(T2[:, sl], T1[:, sl], T1[:, sl], ALU.mult, rd=[T1], wr=[T2])
                pst = k.ps()
                k.mm(pst, pst.ap[:, :n], [(blk, T2[:, sl])], rd=[CST, T2])
                k.ts(T2[:, sl], pst.ap[:, :n], 1e-24, None, ALU.max, None, rd=[pst], wr=[T2])
                k.rsqrt(T2, T2[:, sl], T2[:, sl], 1.0, 0.0, rd=[T2])
                k.tt(KKN[:, sl], T1[:, sl], T2[:, sl], ALU.mult, rd=[T1, T2], wr=[KKN])
                k.ts(T1[:, sl], A[:, sl], vcol(V_KA, i), vcol(V_OMKA, i), ALU.mult, ALU.add, rd=[A, VEC], wr=[T1])
                k.tt(KM[:, sl], KR[:, sl], T1[:, sl], ALU.mult, rd=[KR, T1], wr=[KM])
                k.tt(BP[:, sl], KKN[:, sl], A[:, sl], ALU.mult, rd=[KKN, A], wr=[BP])
                k.tt(T1[:, sl], R[:, sl], KM[:, sl], ALU.mult, rd=[R, KM], wr=[T1])
                k.ts(T1[:, sl], T1[:, sl], vcol(V_RK, i), None, ALU.mult, None, rd=[T1, VEC], wr=[T1])
                pst = k.ps()
                k.mm(pst, pst.ap[:, :n], [(blk, T1[:, sl])], rd=[CST, T1])
                k.tt(BON[:, sl], pst.ap[:, :n], V[:, sl], ALU.mult, rd=[pst, V], wr=[BON])
                k.ts(WS[:, sl], WS[:, sl], -0.6065306597126334, None, ALU.mult, None, rd=[WS], wr=[WS])
                rm = CST[:, 5, :] if False else None
                for c in range(nch):
                    cs = slice(t0 + c * C, t0 + (c + 1) * C)
                    k.op(k.dve, lambda cs=cs: nc.vector.tensor_tensor_scan(
                        T1[:, cs], CST[:, 5, :C], WS[:, cs], 0.0, ALU.mult, ALU.add), rd=[WS, CST], wr=[T1])
                CUM = T1
                k.a(EP[:, sl], CUM[:, sl], AF.Exp, rd=[CUM], wr=[EP])
                a3 = ART[:, 2 * t0:2 * (t0 + n)].rearrange("p (a t b) -> p a t b", t=2, b=C)
                k.tt(a3[:, :, 1, :], v3(R[:, sl]), v3(EP[:, sl]), ALU.mult, rd=[R, EP], wr=[ART])
                k.tt(T2[:, sl], CUM[:, sl], WS[:, sl], ALU.subtract, rd=[CUM, WS], wr=[T2])
                k.a(T2[:, sl], T2[:, sl], AF.Exp, rd=[T2], wr=[T2])
                k.stt(a3[:, :, 0, :], v3(KKN[:, sl]), -1.0, v3(T2[:, sl]), ALU.mult, ALU.mult, rd=[KKN, T2], wr=[ART])
                k.a(T2[:, sl], CUM[:, sl], AF.Exp, rd=[CUM], wr=[T2], scale=-1.0)
                k.tt(KT[:, sl], KM[:, sl], T2[:, sl], ALU.mult, rd=[KM, T2], wr=[KT])
                k.tt(BT[:, sl], BP[:, sl], T2[:, sl], ALU.mult, rd=[BP, T2], wr=[BT])
                c3 = v3(CUM[:, sl])
                k.tt(v3(T2[:, sl]), c3[:, :, C - 1:C].broadcast_to([128, nch, C]), c3, ALU.subtract, rd=[CUM], wr=[T2])
                k.a(T2[:, sl], T2[:, sl], AF.Exp, rd=[T2], wr=[T2])
                k.tt(KPT[:, sl], KM[:, sl], T2[:, sl], ALU.mult, rd=[KM, T2], wr=[KPT])
                k.tt(BPT[:, sl], BP[:, sl], T2[:, sl], ALU.mult, rd=[BP, T2], wr=[BPT])
                k.a(VB[:, sl], V[:, sl], AF.Identity, rd=[V], wr=[VB])

                all_units = list(range(nch))
                ftr = lambda ap, c: ap[:, t0 + c * C:t0 + (c + 1) * C]
                art = lambda c, w, h: ART[64 * h:64 * h + 64, 2 * t0 + c * 2 * C + w * C:2 * t0 + c * 2 * C + (w + 1) * C]
                art2 = lambda c, h: ART[64 * h:64 * h + 64, 2 * t0 + c * 2 * C:2 * t0 + (c + 1) * 2 * C]
                for u0 in range(0, nch if K_RW >= 3 else 0, NCH):
                  units = all_units[u0:u0 + NCH]
                  for c in units:
                      pst = k.ps()
                      pb = pst.ap.bitcast(BF16)[:C, :384].rearrange("p (a b) -> p a b", a=3)
                      for w, src in enumerate((VB, KPT, BPT)):
                          k.tr(pst, pb[:, w, :], ftr(src, c), IDB[:], rd=[src, IDB])
                      k.a(TOK[c % NCH][:C, :, :], pb, AF.Identity, rd=[pst], wr=[TOK[c % NCH]])
                  if K_RW < 4:
                      continue
                  for c in units:
                      I0 = INV[c % NCH][0]
                      for h in range(2):
                          pst = k.ps()
                          pa = pst.ap[:C, :256].rearrange("p (a b) -> p a b", a=2)
                          for w, src in enumerate((KT, BT)):
                              k.mm(pst, pa[:, w, :2 * C], [(src[64 * h:64 * h + 64, t0 + c * C:t0 + (c + 1) * C],
                                                            art2(c, h))], rd=[src, ART])
                          k.tt(AM[c % NCH][:C, 2 * h:2 * h + 2, :2 * C], pa[:, :, :2 * C],
                               MASKA[C].unsqueeze(1).broadcast_to([C, 2, 2 * C]), ALU.mult, rd=[pst, CST], wr=[AM[c % NCH]])
                          pst2 = k.ps()
                          k.mm(pst2, pst2.ap[:C, :C], [(art(c, 0, h), BT[64 * h:64 * h + 64, t0 + c * C:t0 + (c + 1) * C])],
                               rd=[ART, BT])
                          k.tt(I0[:C, h, 2, :C], pst2.ap[:C, :C], MASKL[C], ALU.mult, rd=[pst2, CST], wr=[I0])
                      k.a(I0[:C, :, 0, :C], AM[c % NCH][:C, 1::2, 0:C], AF.Identity, rd=[AM[c % NCH], I0], wr=[I0])
                      k.a(I0[:C, :, 1, :C], IDB[:C, :C].unsqueeze(1).broadcast_to([C, 2, C]), AF.Identity,
                          rd=[IDB, I0], wr=[I0])
                  if K_RW < 5:
                      continue
                  nlev = 6 if C == 64 else 3
                  for lev in range(nlev):
                      last = lev == nlev - 1
                      for c in units:
                          Ia, Ib = INV[c % NCH][lev % 2], INV[c % NCH][(lev + 1) % 2]
                          pst = k.ps()
                          pi = pst.ap[:C, :384].rearrange("p (h w b) -> p h w b", h=2, w=3)
                          for h in range(2):
                              for w in range(2):
                                  k.mm(pst, pi[:, h, w, :C], [(Ia[:C, h, 2, :C], Ia[:C, h, w, :C])], rd=[Ia])
                              if not last:
                                  k.mm(pst, pi[:, h, 2, :C], [(Ia[:C, h, 0, :C], Ia[:C, h, 2, :C])], rd=[Ia])
                          if not last:
                              k.a(Ib[:C, :, 0, :C], pi[:, :, 0, :C], AF.Identity, rd=[pst], wr=[Ib])
                              k.a(Ib[:C, :, 2, :C], pi[:, :, 2, :C], AF.Identity, rd=[pst, Ib], wr=[Ib])
                          k.tt(Ib[:C, :, 1, :C], pi[:, :, 1, :C], Ia[:C, :, 1, :C], ALU.add, rd=[pst, Ia, Ib], wr=[Ib])
                  XI = nlev % 2
                  if K_RW < 6:
                      continue
                  for c in units:
                      Hs = HS[:, i, :] if ti == 0 else SW[:, c, :]
                      Ht = HS if ti == 0 else SW
                      k.a(HB[:], Hs, AF.Identity, rd=[Ht], wr=[HB])
                      X = INV[c % NCH][XI]
                      if K_RW < 7:
                          continue
                      pc = k.ps()
                      pwc = pc.ap[:C, :128].rearrange("p (a b) -> p a b", a=2)
                      for h in range(2):
                          k.mm(pc, pwc[:, h, :], [(AM[c % NCH][:C, 2 * h, 0:C], TOK[c % NCH][:C, 0, 64 * h:64 * h + 64])],
                               rd=[AM[c % NCH], TOK[c % NCH]])
                      phs = [k.ps(), k.ps()]
                      for h in range(2):
                          k.mm(phs[h], phs[h].ap[:C, :64], [(art(c, 0, h), HB[64 * h:64 * h + 64, :])], rd=[ART, HB])
                      k.a(WT[:C], pwc, AF.Identity, rd=[pc], wr=[WT])
                      for h in range(2):
                          k.tt(WB[:C, h, :], WT[:C, h, :], phs[h].ap[:C, :64], ALU.add, rd=[WT, phs[h]], wr=[WB])
                      if K_RW < 8:
                          continue
                      pst = k.ps()
                      pu = pst.ap[:C, :128].rearrange("p (a b) -> p a b", a=2)
                      for h in range(2):
                          k.mm(pst, pu[:, h, :], [(X[:C, h, 1, :C], WB[:C, h, :])], rd=[X, WB])
                      k.a(UB[:C], pu, AF.Identity, rd=[pst], wr=[UB])
                      if K_RW < 9:
                          continue
                      pc = k.ps()
                      pyc = pc.ap[:C, :128].rearrange("p (a b) -> p a b", a=2)
                      for h in range(2):
                          k.mm(pc, pyc[:, h, :], [(AM[c % NCH][:C, 2 * h, C:2 * C], TOK[c % NCH][:C, 0, 64 * h:64 * h + 64]),
                                                  (AM[c % NCH][:C, 2 * h + 1, C:2 * C], UB[:C, h, :])],
                               rd=[AM[c % NCH], TOK[c % NCH], UB])
                      phs = [k.ps(), k.ps()]
                      for h in range(2):
                          k.mm(phs[h], phs[h].ap[:C, :64], [(art(c, 1, h), HB[64 * h:64 * h + 64, :])], rd=[ART, HB])
                      k.a(YTK[:C, :], pc.ap[:C, :128], AF.Identity, rd=[pc], wr=[YTK])
                      for h in range(2):
                          k.tt(YTK[:C, 64 * h:64 * h + 64], YTK[:C, 64 * h:64 * h + 64], phs[h].ap[:C, :64], ALU.add,
                               rd=[YTK, phs[h]], wr=[YTK])
                      if K_RW < 10:
                          continue
                      pst = k.ps()
                      k.tr(pst, pst.ap[:, :C], YTK[:C, :], ident[:C, :C], rd=[YTK, CST])
                      k.a(YF[:, t0 + c * C:t0 + (c + 1) * C], pst.ap[:, :C], AF.Identity, rd=[pst], wr=[YF])
                      pst = k.ps()
                      for h in range(2):
                          k.mm(pst, pst.ap[64 * h:64 * h + 64, :64],
                               [(TOK[c % NCH][:C, 1, 64 * h:64 * h + 64], TOK[c % NCH][:C, 0, 64 * h:64 * h + 64]),
                                (TOK[c % NCH][:C, 2, 64 * h:64 * h + 64], UB[:C, h, :])], rd=[TOK[c % NCH], UB])
                      gc = t0 + c * C + C - 1
                      k.stt(Hs, Hs, EP[:, gc:gc + 1], pst.ap[:, :64], ALU.mult, ALU.add, rd=[Ht, EP, pst], wr=[Ht])
                if K_RW < 11:
                    continue
                pst = k.ps()
                k.mm(pst, pst.ap[:, :n], [(blk, YF[:, sl])], rd=[CST, YF])
                k.stt(T1[:, sl], pst.ap[:, :n], -1.0 / 64, YF[:, sl], ALU.mult, ALU.add, rd=[pst, YF], wr=[T1])
                k.tt(T2[:, sl], T1[:, sl], T1[:, sl], ALU.mult, rd=[T1], wr=[T2])
                pst = k.ps()
                k.mm(pst, pst.ap[:, :n], [(blk, T2[:, sl])], rd=[CST, T2])
                k.rsqrt(T2, T2[:, sl], pst.ap[:, :n], 1.0 / 64, GN_EPS, rd=[pst])
                k.tt(T1[:, sl], T1[:, sl], T2[:, sl], ALU.mult, rd=[T1, T2], wr=[T1])
                k.ts(T1[:, sl], T1[:, sl], vcol(V_LNW, i), vcol(V_LNB, i), ALU.mult, ALU.add, rd=[T1, VEC], wr=[T1])
                k.tt(T1[:, sl], T1[:, sl], BON[:, sl], ALU.add, rd=[T1, BON], wr=[T1])
                k.tt(YG[:, i, sl], T1[:, sl], G[:, sl], ALU.mult, rd=[T1, G], wr=[YG])
            k.dma(k.sp, o_wkvs[p, i], SW[:], rd=[SW])
        if p == NPASS - 1:
            k.dma(k.sp, o_wkvp, HS[:], rd=[HS])

        def ev_out(gi, ti, outs):
            t0, n = tiles[ti]
            k.a(Y[:, gi, t0:t0 + n], outs[0][1], AF.Identity, rd=[outs[0][0]], wr=[Y, HX])
        gemm(rw_wo[0], D, [[i * 128] for i in range(KC)],
             [(YG, lambda kd, t0, n: YG[:, kd, t0:t0 + n])], ev_out)
        postnorm(l, s, p, Y, RSTD, SQ, TMP)
        k.barrier()


    for p in range(NPASS):
        k.dma(k.sp, XT[:, :, 0:TPR], xp[p], wr=[XT])
        k.dma(k.sp, XT[:, :, TPR:TP], xs[p], wr=[XT])
        nsub = 0
        for l in range(2):
            for s in range(3):
                if nsub >= DBG_NSUB:
                    break
                if s == 0:
                    ffn(l, 0, 0, p)
                elif s == 2:
                    ffn(l, 1, 2, p)
                elif l == 0:
                    conv(p)
                else:
                    rwkv(p)
                nsub += 1
        k.dma(k.sp, yp[p], XT[:, :, 0:TPR], rd=[XT])
        k.dma(k.sp, ys[p], XT[:, :, TPR:TP], rd=[XT])
        k.barrier()
    k.barrier()
    return nc


_NC = None


def _fm(a):
    r = a.shape[0]
    return np.ascontiguousarray(a.reshape(r, KC, 128).transpose(2, 1, 0))


def _fm_inv(a):
    r = a.shape[2]
    return np.ascontiguousarray(a.transpose(2, 1, 0).reshape(r, D))


def _consts():
    c = np.zeros((128, 6, 128), np.float32)
    c[:, 0, :] = np.eye(128, dtype=np.float32)
    hh = np.arange(128) // 64
    c[:, 1, :] = (hh[:, None] == hh[None, :]).astype(np.float32)
    s = np.arange(64)[:, None]; t = np.arange(64)[None, :]
    c[:64, 2, 0:64] = (s < t); c[:64, 2, 64:128] = (s <= t)
    s8 = np.arange(8)[:, None]; t8 = np.arange(8)[None, :]
    c[:8, 3, 0:8] = (s8 < t8); c[:8, 3, 8:16] = (s8 <= t8)
    c[:64, 4, 0:64] = (s > t)
    c[:8, 4, 64:72] = (s8 > t8)
    c[:, 5, :] = 1.0
    return c


def kernel(**inp):
    global _NC
    f = lambda n: np.asarray(inp[n], dtype=np.float32)
    x_prompt, x_sample = f("x_prompt"), f("x_sample")
    state_conv, state_shift, state_wkv = f("state_conv"), f("state_shift"), f("state_wkv")
    c_prompt, c_sample = f("c_prompt"), f("c_sample")
    if _NC is None:
        _NC = build()
    nc = _NC
    vec = np.zeros((128, NVEC, KC), np.float32)
    put = lambda idx, v: vec.__setitem__((slice(None), idx), v.reshape(KC, 128).T)
    for l in range(2):
        for s in range(3):
            put(V_NPRE + l * 3 + s, f("norm_pre")[l, s]); put(V_NPOST + l * 3 + s, f("norm_post")[l, s])
    for j in range(3):
        put(V_CW + j, f("conv_w")[0, j])
    for j in range(6):
        put(V_MIX + j, f("rw_mix")[0, j])
    put(V_W0, f("rw_w0")[0]); put(V_A0, f("rw_a0")[0]); put(V_KK, f("rw_kk")[0]); put(V_KA, f("rw_ka")[0])
    put(V_RK, f("rw_rk")[0].reshape(-1)); put(V_LNW, f("rw_lnw")[0]); put(V_LNB, f("rw_lnb")[0])
    modb = np.ascontiguousarray(f("mod_b").reshape(2, 144, 128).transpose(2, 0, 1))
    cst = _consts()
    shared = {n: f(n) for n in ("mod_w", "ffn_w_in", "ffn_w_out", "conv_w_in", "conv_w_out", "rw_w1", "rw_w2",
                                 "rw_a1", "rw_a2", "rw_g1", "rw_g2", "rw_wr", "rw_wk", "rw_wv", "rw_wo")}
    in_maps = []
    for c in range(NCORES):
        sp_ = c % 4
        m = dict(shared)
        m["vec"] = vec; m["modb"] = modb; m["cst"] = cst
        m["xp"] = np.stack([_fm(x_prompt[sp_, p * TPR:(p + 1) * TPR]) for p in range(NPASS)])
        xs_c = x_sample[16 * c:16 * c + 16]
        m["xs"] = np.stack([_fm(xs_c[NSS * p:NSS * p + NSS].reshape(TSM, D)) for p in range(NPASS)])
        cc = np.concatenate([c_prompt[sp_:sp_ + 1], c_sample[16 * c:16 * c + 16]], 0)
        m["cT"] = _fm(cc)
        sc = state_conv[0, 16 * c:16 * c + 16]
        m["sconv"] = np.stack([_fm(sc[NSS * p:NSS * p + NSS].reshape(NSS * 2, D)).reshape(128, KC, NSS, 2)
                               for p in range(NPASS)])
        ss = state_shift[0, 16 * c:16 * c + 16]
        m["sshift"] = np.stack([_fm(ss[NSS * p:NSS * p + NSS]) for p in range(NPASS)])
        sw = state_wkv[0, 16 * c:16 * c + 16]
        swh = sw.reshape(NPASS, NSS, KC, 2, 64, 64).transpose(0, 2, 3, 5, 1, 4).reshape(NPASS, KC, 128, NSS, 64)
        m["swkv"] = np.ascontiguousarray(swh)
        in_maps.append(m)
    res = run_bass_kernel_spmd(nc, in_maps, core_ids=list(range(NCORES)))
    R = res.results
    y_prompt = np.zeros((4, 2048, D), np.float32); y_sample = np.zeros((128, 8, D), np.float32)
    conv_p = np.zeros((1, 4, 2, D), np.float32); shift_p = np.zeros((1, 4, D), np.float32)
    wkv_p = np.zeros((1, 4, 32, 64, 64), np.float32)
    conv_s = np.zeros((1, 128, 2, D), np.float32); shift_s = np.zeros((1, 128, D), np.float32)
    wkv_s = np.zeros((1, 128, 32, 64, 64), np.float32)
    for c in range(NCORES):
        r = R[c]
        if c < 4:
            for p in range(NPASS):
                y_prompt[c, p * TPR:(p + 1) * TPR] = _fm_inv(r["yp"][p])
            conv_p[0, c] = _fm_inv(r["o_convp"])
            shift_p[0, c] = _fm_inv(r["o_shiftp"][:, :, None])[0]
            wkv_p[0, c] = r["o_wkvp"].reshape(2, 64, KC, 64).transpose(2, 0, 3, 1).reshape(32, 64, 64)
        for p in range(NPASS):
            q = 16 * c + NSS * p
            y_sample[q:q + NSS] = _fm_inv(r["ys"][p]).reshape(NSS, 8, D)
            conv_s[0, q:q + NSS] = _fm_inv(r["o_convs"][p].reshape(128, KC, NSS * 2)).reshape(NSS, 2, D)
            shift_s[0, q:q + NSS] = _fm_inv(r["o_shifts"][p])
            wkv_s[0, q:q + NSS] = r["o_wkvs"][p].reshape(KC, 2, 64, NSS, 64).transpose(3, 0, 1, 4, 2).reshape(NSS, 32, 64, 64)
    return (y_prompt, y_sample, conv_p, shift_p, wkv_p, conv_s, shift_s, wkv_s)


if __name__ == "__main__":
    import time
    t0 = time.time()
    nc = build()
    print("built in", time.time() - t0)
```

```python
import os
import numpy as np
import concourse.bass as bass
import concourse.mybir as mybir
from concourse.bass_utils import run_bass_kernel_spmd

F32, BF16 = mybir.dt.float32, mybir.dt.bfloat16
AF = mybir.ActivationFunctionType
ALU = mybir.AluOpType

D = 2048; KC = 16; DFF = 5632; FC = 44
NPASS = 4; TPR = 512; NSS = 4; TSM = NSS * 8; TP = TPR + TSM
NSEQ = 17
NCORES = int(os.environ.get('K_NCORES', '8'))
RMS_EPS = 1e-6; GN_EPS = 64e-5
NRING = 5
NDS = 24
V_NPRE = 0
V_NPOST = 6
V_CW = 12
V_MIX = 15
V_W0 = 21; V_A0 = 22; V_KK = 23; V_KA = 24; V_RK = 25; V_LNW = 26; V_LNB = 27; V_OMKA = 28
NVEC = 29
DBG_NSUB = int(os.environ.get("K_NSUB", "6"))
K_RW = int(os.environ.get("K_RW", "99"))
K_SELF = int(os.environ.get("K_SELF", "2"))


class Eng:
    def __init__(self, nc, name, h):
        self.h = h
        self.name = name
        self.sem = nc.alloc_semaphore("c_" + name)
        self.cnt = 0
        self.seen = {}
        self.dsems = [nc.alloc_semaphore("d_%s%d" % (name, i)) for i in range(NDS)] if name in ("pool", "sp") else []
        self.dvals = [0] * NDS
        self.di = 0


class T:
    __slots__ = ("ap", "w", "r")

    def __init__(self, ap):
        self.ap = ap
        self.w = None
        self.r = {}

    def __getitem__(self, k):
        return self.ap[k]


class K:
    def __init__(self):
        nc = bass.Bass("TRN2", target_bir_lowering=False)
        self.nc = nc
        self.pe = Eng(nc, "pe", nc.tensor)
        self.act = Eng(nc, "act", nc.scalar)
        self.dve = Eng(nc, "dve", nc.vector)
        self.pool = Eng(nc, "pool", nc.gpsimd)
        self.sp = Eng(nc, "sp", nc.sync)
        self.engs = [self.pe, self.act, self.dve, self.pool, self.sp]
        self.psb = [T(nc.alloc_psum_tensor("ps%d" % i, [128, 512], F32).ap()) for i in range(8)]
        self.psi = 0
        self.ringi = 0

    def _deps(self, e, rd, wr):
        deps = {}
        for t in rd:
            if t.w is not None:
                s, v = t.w
                if deps.get(s, 0) < v:
                    deps[s] = v
        for t in wr:
            if t.w is not None:
                s, v = t.w
                if deps.get(s, 0) < v:
                    deps[s] = v
            for s, v in t.r.items():
                if deps.get(s, 0) < v:
                    deps[s] = v
        for s, v in deps.items():
            if e.seen.get(s, 0) >= v:
                continue
            if s is e.sem and (K_SELF == 0 or (K_SELF == 2 and e.name == 'pe')):
                continue
            e.h.wait_ge(s, v)
            e.seen[s] = v

    def op(self, e, fn, rd=(), wr=()):
        self._deps(e, rd, wr)
        ins = fn()
        e.cnt += 1
        ins.then_inc(e.sem, 1)
        for t in rd:
            if t.r.get(e.sem, 0) < e.cnt:
                t.r[e.sem] = e.cnt
        for t in wr:
            t.w = (e.sem, e.cnt)
            t.r = {}

    def dma(self, q, out_ap, in_ap, rd=(), wr=()):
        slot = q.di % NDS
        q.di += 1
        sem = q.dsems[slot]
        prev = q.dvals[slot]
        if prev and q.seen.get(sem, 0) < prev:
            q.h.wait_ge(sem, prev)
            q.seen[sem] = prev
        self._deps(q, rd, wr)
        ins = q.h.dma_start(out=out_ap, in_=in_ap)
        q.dvals[slot] = prev + 16
        ins.then_inc(sem, 16)
        v = prev + 16
        for t in rd:
            t.r[sem] = v
        for t in wr:
            t.w = (sem, v)
            t.r = {}

    def barrier(self, final=False):
        for e in self.engs:
            for o in self.engs:
                if o is e:
                    continue
                if o.cnt > e.seen.get(o.sem, 0):
                    e.h.wait_ge(o.sem, o.cnt)
                    e.seen[o.sem] = o.cnt
            for q in (self.pool, self.sp):
                for i in range(NDS):
                    v = q.dvals[i]
                    if v and e.seen.get(q.dsems[i], 0) < v:
                        e.h.wait_ge(q.dsems[i], v)
                        e.seen[q.dsems[i]] = v

    def ps(self):
        t = self.psb[self.psi % 8]
        self.psi += 1
        return t

    def mm(self, out_t, out_ap, pairs, rd):
        def fn():
            n = len(pairs)
            ins = None
            for i, (l, r) in enumerate(pairs):
                ins = self.nc.tensor.matmul(out_ap, l, r, start=(i == 0), stop=(i == n - 1))
            return ins
        self.op(self.pe, fn, rd=rd, wr=[out_t])

    def tr(self, out_t, out_ap, in_ap, ident_ap, rd):
        self.op(self.pe, lambda: self.nc.tensor.transpose(out_ap, in_ap, ident_ap), rd=rd, wr=[out_t])

    def a(self, out_ap, in_ap, func, rd, wr, bias=None, scale=1.0):
        if bias is None:
            f = lambda: self.nc.scalar.activation(out=out_ap, in_=in_ap, func=func, scale=scale)
        else:
            f = lambda: self.nc.scalar.activation(out=out_ap, in_=in_ap, func=func, bias=bias, scale=scale)
        self.op(self.act, f, rd=rd, wr=wr)

    def tt(self, out_ap, a_ap, b_ap, opx, rd, wr):
        self.op(self.dve, lambda: self.nc.vector.tensor_tensor(out_ap, a_ap, b_ap, opx), rd=rd, wr=wr)

    def ts(self, out_ap, a_ap, s1, s2, op0, op1, rd, wr):
        if s2 is None:
            f = lambda: self.nc.vector.tensor_scalar(out_ap, a_ap, s1, None, op0)
        else:
            f = lambda: self.nc.vector.tensor_scalar(out_ap, a_ap, s1, s2, op0, op1)
        self.op(self.dve, f, rd=rd, wr=wr)

    def stt(self, out_ap, a_ap, sc, b_ap, op0, op1, rd, wr):
        self.op(self.dve, lambda: self.nc.vector.scalar_tensor_tensor(out_ap, a_ap, sc, b_ap, op0, op1),
                rd=rd, wr=wr)

    def rsqrt(self, out_t, out_ap, in_ap, scale, bias, rd):
        self.a(out_ap, in_ap, AF.Sqrt, rd=rd, wr=[out_t], bias=bias, scale=scale)
        self.op(self.dve, lambda: self.nc.vector.reciprocal(out_ap, out_ap), rd=[out_t], wr=[out_t])

    def cp(self, out_ap, in_ap, rd, wr):
        self.op(self.dve, lambda: self.nc.vector.tensor_copy(out_ap, in_ap), rd=rd, wr=wr)


def build():
    k = K()
    nc = k.nc
    di = lambda name, shape: nc.dram_tensor(name, shape, F32, kind="ExternalInput").ap()
    do = lambda name, shape: nc.dram_tensor(name, shape, F32, kind="ExternalOutput").ap()
    xp = di("xp", [NPASS, 128, KC, TPR]); xs = di("xs", [NPASS, 128, KC, TSM])
    cT = di("cT", [128, KC, NSEQ])
    sconv = di("sconv", [NPASS, 128, KC, NSS, 2]); sshift = di("sshift", [NPASS, 128, KC, NSS])
    swkv = di("swkv", [NPASS, KC, 64, NSS, 2, 64])
    vec = di("vec", [128, NVEC, KC]); modb = di("modb", [128, 2, 144])
    cst = di("cst", [128, 6, 128])
    mod_w = di("mod_w", [2, D, 9 * D])
    ffn_w_in = di("ffn_w_in", [2, 2, D, 2 * DFF]); ffn_w_out = di("ffn_w_out", [2, 2, DFF, D])
    conv_w_in = di("conv_w_in", [1, D, 3 * D]); conv_w_out = di("conv_w_out", [1, D, D])
    rw_w1 = di("rw_w1", [1, D, 96]); rw_w2 = di("rw_w2", [1, 96, D])
    rw_a1 = di("rw_a1", [1, D, 96]); rw_a2 = di("rw_a2", [1, 96, D])
    rw_g1 = di("rw_g1", [1, D, 256]); rw_g2 = di("rw_g2", [1, 256, D])
    rw_wr = di("rw_wr", [1, D, D]); rw_wk = di("rw_wk", [1, D, D]); rw_wv = di("rw_wv", [1, D, D])
    rw_wo = di("rw_wo", [1, D, D])
    yp = do("yp", [NPASS, 128, KC, TPR]); ys = do("ys", [NPASS, 128, KC, TSM])
    o_convp = do("o_convp", [128, KC, 2]); o_shiftp = do("o_shiftp", [128, KC])
    o_wkvp = do("o_wkvp", [64, KC, 2, 64])
    o_convs = do("o_convs", [NPASS, 128, KC, NSS, 2]); o_shifts = do("o_shifts", [NPASS, 128, KC, NSS])
    o_wkvs = do("o_wkvs", [NPASS, KC, 64, NSS, 2, 64])

    sb = lambda name, shape, dt=F32: T(nc.alloc_sbuf_tensor(name, shape, dt).ap())
    XT = sb("XT", [128, KC, TP])
    MOD = sb("MOD", [128, 2, 144, NSEQ])
    VEC = sb("VEC", [128, NVEC, KC])
    CST = sb("CST", [128, 6, 128])
    IDB = sb("IDB", [128, 128], BF16); ONB = sb("ONB", [128, 128], BF16)
    UC = sb("UC", [128, KC, 2]); HC = sb("HC", [128, KC]); HS = sb("HS", [64, KC, 2, 64])
    RING = [sb("ring%d" % i, [128, 2048], BF16) for i in range(NRING)]
    ARENA_B = 120 * 1024
    ARENA = nc.alloc_sbuf_tensor("arena", [128, ARENA_B // 4], F32).ap()

    class Carver:
        def __init__(self):
            self.off = 0

        def get(self, shape, dt=F32):
            n = int(np.prod(shape[1:]))
            nb = n * (4 if dt == F32 else 2)
            nb4 = (nb + 31) // 32 * 32
            assert self.off + nb4 <= ARENA_B, ("arena overflow", self.off, nb4)
            w0 = self.off // 4
            ap = ARENA[:shape[0], w0:w0 + nb4 // 4]
            if dt != F32:
                ap = ap.bitcast(dt)
            ap = ap[:, :n]
            if len(shape) == 3:
                ap = ap.rearrange("p (a b) -> p a b", a=shape[1])
            elif len(shape) == 4:
                ap = ap.rearrange("p (a b c) -> p a b c", a=shape[1], b=shape[2])
            self.off += nb4
            return T(ap)

    ident = CST[:, 0, :]; blk = CST[:, 1, :]
    vcol = lambda vi, kd: VEC[:, vi, kd:kd + 1]

    k.dma(k.sp, VEC[:], vec, wr=[VEC])
    k.dma(k.sp, CST[:], cst, wr=[CST])
    k.ts(VEC[:, V_OMKA, :], VEC[:, V_KA, :], -1.0, 1.0, ALU.mult, ALU.add, rd=[VEC], wr=[VEC])
    k.dma(k.pool, IDB[:], cst[:, 0, :], wr=[IDB])
    k.dma(k.pool, ONB[:], cst[:, 5, :], wr=[ONB])
    k.op(k.dve, lambda: nc.vector.memset(UC[:], 0.0), wr=[UC])
    k.op(k.dve, lambda: nc.vector.memset(HC[:], 0.0), wr=[HC])
    k.op(k.dve, lambda: nc.vector.memset(HS[:], 0.0), wr=[HS])

    tiles = [(0, TPR), (TPR, TSM)]

    def gemm(W, krows, groups, acts, evac, mix=None, ncols=128):
        nk = (krows + 127) // 128
        for gi, cols in enumerate(groups):
            per_col = []
            for c0 in cols:
                sl = []
                kk0 = 0
                while kk0 < nk:
                    nkk = min(16, nk - kk0)
                    rt = RING[k.ringi % NRING]; k.ringi += 1
                    rows = min(128, krows - kk0 * 128) if nk == 1 else 128
                    dst = rt.ap[:rows, :nkk * ncols].rearrange("p (a b) -> p a b", a=nkk)
                    if nk == 1:
                        src = W[0:rows, c0:c0 + ncols].rearrange("(a p) f -> p a f", a=1)
                    else:
                        src = W[kk0 * 128:(kk0 + nkk) * 128, c0:c0 + ncols].rearrange("(a p) f -> p a f", p=128)
                    k.dma(k.pool, dst, src, wr=[rt])
                    rs = None
                    if mix is not None:
                        rs = RING[k.ringi % NRING]; k.ringi += 1
                        dst2 = rs.ap[:rows, :nkk * ncols].rearrange("p (a b) -> p a b", a=nkk)
                        mv = VEC[:, mix, kk0:kk0 + nkk].unsqueeze(2).broadcast_to([128, nkk, ncols])
                        k.tt(dst2, dst, mv, ALU.mult, rd=[rt, VEC], wr=[rs])
                    sl.append((kk0, nkk, rows, rt, rs))
                    kk0 += nkk
                per_col.append(sl)
            for ti, (t0, n) in enumerate(tiles):
                outs = []
                for ci, c0 in enumerate(cols):
                    pst = k.ps()
                    pairs = []; rd = []
                    for (kk0, nkk, rows, rt, rs) in per_col[ci]:
                        for j in range(nkk):
                            pairs.append((rt.ap[:rows, j * ncols:(j + 1) * ncols], acts[0][1](kk0 + j, t0, n)))
                            if rs is not None:
                                pairs.append((rs.ap[:rows, j * ncols:(j + 1) * ncols], acts[1][1](kk0 + j, t0, n)))
                        rd.append(rt)
                        if rs is not None:
                            rd.append(rs)
                    rd += [a_[0] for a_ in acts]
                    k.mm(pst, pst.ap[:ncols, :n], pairs, rd)
                    outs.append((pst, pst.ap[:ncols, :n]))
                evac(gi, ti, outs)

    cv = Carver()
    CS = cv.get([128, KC, NSEQ]); SCB = cv.get([128, KC, NSEQ], BF16); SG = cv.get([128, KC, NSEQ])
    k.dma(k.sp, CS[:], cT, wr=[CS])
    k.a(SG[:], CS[:], AF.Sigmoid, rd=[CS], wr=[SG])
    k.tt(SCB[:], CS[:], SG[:], ALU.mult, rd=[CS, SG], wr=[SCB])
    MB = cv.get([128, 2, 144])
    k.dma(k.sp, MB[:], modb, wr=[MB])
    mod_tiles = [(0, NSEQ)]
    for l in range(2):
        def ev(gi, ti, outs, l=l):
            for ci, (pst, pa) in enumerate(outs):
                f = gi * 2 + ci
                k.a(MOD[:, l, f, :], pa, AF.Identity, rd=[pst, MB], wr=[MOD], bias=MB[:, l, f:f + 1])
        old_tiles = tiles
        tiles = mod_tiles
        gemm(mod_w[l], D, [[f * 128, (f + 1) * 128] for f in range(0, 144, 2)],
             [(SCB, lambda kd, t0, n: SCB[:, kd, :])], ev)
        tiles = old_tiles
        for s in range(3):
            sc = MOD[:, l, (s * 3 + 1) * 16:(s * 3 + 2) * 16, :]
            gt = MOD[:, l, (s * 3 + 2) * 16:(s * 3 + 3) * 16, :]
            gpre = VEC[:, V_NPRE + l * 3 + s, :].unsqueeze(2).broadcast_to([128, KC, NSEQ])
            gpost = VEC[:, V_NPOST + l * 3 + s, :].unsqueeze(2).broadcast_to([128, KC, NSEQ])
            k.stt(sc, sc, 1.0, gpre, ALU.add, ALU.mult, rd=[MOD, VEC], wr=[MOD])
            k.stt(gt, gt, (1.0 if s == 1 else 0.5), gpost, ALU.mult, ALU.mult, rd=[MOD, VEC], wr=[MOD])
    k.barrier()

    def modv(l, s, kind, kd, seq0, nseq):
        return MOD[:, l, (s * 3 + kind) * 16 + kd, seq0:seq0 + nseq]

    def sumsq_rstd(src, RSTD, SQ, eps, p):
        for (t0, n) in tiles:
            pst = k.ps()
            for kd in range(KC):
                sq = SQ[kd % 2]
                k.a(sq[:, :n], src[:, kd, t0:t0 + n], AF.Square, rd=[src], wr=[sq])
                k.op(k.pe, lambda sq=sq, kd=kd, pst=pst, n=n: nc.tensor.matmul(
                    pst.ap[:, :n], ONB[:], sq[:, :n], start=(kd == 0), stop=(kd == KC - 1)),
                    rd=[sq, ONB], wr=[pst])
            k.rsqrt(RSTD, RSTD[:, t0:t0 + n], pst.ap[:, :n], 1.0 / D, eps, rd=[pst])

    def prenorm(l, s, p, RSTD, SQ, TMP, sink):
        sumsq_rstd(XT, RSTD, SQ, RMS_EPS, p)
        for kd in range(KC):
            for ti, (t0, n) in enumerate(tiles):
                tmp = TMP[(kd * 2 + ti) % 2]
                if ti == 0:
                    k.stt(tmp[:, :n], XT[:, kd, t0:t0 + n], modv(l, s, 1, kd, 0, 1), RSTD[:, t0:t0 + n],
                          ALU.mult, ALU.mult, rd=[XT, MOD, RSTD], wr=[tmp])
                    k.a(tmp[:, :n], tmp[:, :n], AF.Identity, rd=[tmp, MOD], wr=[tmp], bias=modv(l, s, 0, kd, 0, 1))
                else:
                    q0 = 1 + NSS * p
                    t3 = tmp[:, :n].rearrange("p (a b) -> p a b", a=NSS)
                    k.tt(tmp[:, :n], XT[:, kd, t0:t0 + n], RSTD[:, t0:t0 + n], ALU.mult, rd=[XT, RSTD], wr=[tmp])
                    k.tt(t3, t3, modv(l, s, 1, kd, q0, NSS).unsqueeze(2).broadcast_to([128, NSS, 8]), ALU.mult,
                         rd=[tmp, MOD], wr=[tmp])
                    k.tt(t3, t3, modv(l, s, 0, kd, q0, NSS).unsqueeze(2).broadcast_to([128, NSS, 8]), ALU.add,
                         rd=[tmp, MOD], wr=[tmp])
                sink(kd, ti, t0, n, tmp)

    def postnorm(l, s, p, Y, RSTD, SQ, TMP):
        sumsq_rstd(Y, RSTD, SQ, RMS_EPS, p)
        for kd in range(KC):
            for ti, (t0, n) in enumerate(tiles):
                tmp = TMP[(kd * 2 + ti) % 2]
                if ti == 0:
                    k.stt(tmp[:, :n], Y[:, kd, t0:t0 + n], modv(l, s, 2, kd, 0, 1), RSTD[:, t0:t0 + n],
                          ALU.mult, ALU.mult, rd=[Y, MOD, RSTD], wr=[tmp])
                else:
                    q0 = 1 + NSS * p
                    t3 = tmp[:, :n].rearrange("p (a b) -> p a b", a=NSS)
                    k.tt(tmp[:, :n], Y[:, kd, t0:t0 + n], RSTD[:, t0:t0 + n], ALU.mult, rd=[Y, RSTD], wr=[tmp])
                    k.tt(t3, t3, modv(l, s, 2, kd, q0, NSS).unsqueeze(2).broadcast_to([128, NSS, 8]), ALU.mult,
                         rd=[tmp, MOD], wr=[tmp])
                k.tt(XT[:, kd, t0:t0 + n], XT[:, kd, t0:t0 + n], tmp[:, :n], ALU.add, rd=[XT, tmp], wr=[XT])

    def ffn(l, which, s, p):
        cv = Carver()
        YH = cv.get([128, KC, TP])
        Hh = T(YH.ap.rearrange("p a b -> p (a b)").bitcast(BF16)[:, :KC * TP].rearrange("p (a b) -> p a b", a=KC))
        AT = cv.get([128, FC, TP], BF16)
        RSTD = cv.get([128, TP]); SQ = [cv.get([128, 512], BF16) for _ in range(2)]
        TMP = [cv.get([128, 512]) for _ in range(2)]
        SGT = [cv.get([128, 512]) for _ in range(2)]

        def sink(kd, ti, t0, n, tmp):
            k.a(Hh[:, kd, t0:t0 + n], tmp[:, :n], AF.Identity, rd=[tmp], wr=[Hh])
        prenorm(l, s, p, RSTD, SQ, TMP, sink)

        cnt = [0]

        def ev_in(gi, ti, outs):
            t0, n = tiles[ti]
            (pg, ga), (pu, ua) = outs
            sg = SGT[cnt[0] % 2]; cnt[0] += 1
            k.a(sg[:, :n], ga, AF.Silu, rd=[pg], wr=[sg])
            k.tt(AT[:, gi, t0:t0 + n], sg[:, :n], ua, ALU.mult, rd=[sg, pu], wr=[AT])
        gemm(ffn_w_in[l, which], D, [[j * 128, DFF + j * 128] for j in range(FC)],
             [(Hh, lambda kd, t0, n: Hh[:, kd, t0:t0 + n])], ev_in)

        def ev_out(gi, ti, outs):
            t0, n = tiles[ti]
            k.a(YH[:, gi, t0:t0 + n], outs[0][1], AF.Identity, rd=[outs[0][0]], wr=[YH, Hh])
        gemm(ffn_w_out[l, which], DFF, [[i * 128] for i in range(KC)],
             [(AT, lambda kd, t0, n: AT[:, kd, t0:t0 + n])], ev_out)
        postnorm(l, s, p, YH, RSTD, SQ, TMP)
        k.barrier()

    def conv(p):
        l, s = 0, 1
        cv = Carver()
        Hh = cv.get([128, KC, TP], BF16); BZ = cv.get([128, KC, TP], BF16); Y = cv.get([128, KC, TP])
        RSTD = cv.get([128, TP]); SQ = [cv.get([128, 512], BF16) for _ in range(2)]
        TMP = [cv.get([128, 512]) for _ in range(2)]
        UT = cv.get([128, 2 + TPR]); CT = cv.get([128, 512]); ZT = cv.get([128, 512])
        US = cv.get([128, NSS, 10])
        SCV = cv.get([128, KC, NSS, 2]); CSO = cv.get([128, KC, NSS, 2])
        k.dma(k.sp, SCV[:], sconv[p], wr=[SCV])

        def sink(kd, ti, t0, n, tmp):
            k.a(Hh[:, kd, t0:t0 + n], tmp[:, :n], AF.Identity, rd=[tmp], wr=[Hh])
        prenorm(l, s, p, RSTD, SQ, TMP, sink)
        cw = lambda i, j: VEC[:, V_CW + j, i:i + 1]

        def ev(i, ti, outs):
            t0, n = tiles[ti]
            (pb, ba), (pc, ca), (px, xa) = outs
            k.a(CT[:, :n], ca, AF.Identity, rd=[pc], wr=[CT])
            if ti == 0:
                k.a(UT[:, 0:2], UC[:, i, :], AF.Identity, rd=[UC], wr=[UT])
                k.tt(UT[:, 2:2 + n], CT[:, :n], xa, ALU.mult, rd=[CT, px, UT], wr=[UT])
                k.ts(ZT[:, :n], UT[:, 0:n], cw(i, 0), None, ALU.mult, None, rd=[UT, VEC], wr=[ZT])
                k.stt(ZT[:, :n], UT[:, 1:n + 1], cw(i, 1), ZT[:, :n], ALU.mult, ALU.add, rd=[UT, VEC, ZT], wr=[ZT])
                k.stt(ZT[:, :n], UT[:, 2:n + 2], cw(i, 2), ZT[:, :n], ALU.mult, ALU.add, rd=[UT, VEC, ZT], wr=[ZT])
                k.tt(BZ[:, i, t0:t0 + n], ZT[:, :n], ba, ALU.mult, rd=[ZT, pb], wr=[BZ])
                k.a(UC[:, i, :], UT[:, n:n + 2], AF.Identity, rd=[UT], wr=[UC])
            else:
                v3 = lambda ap: ap.rearrange("p (a b) -> p a b", a=NSS)
                k.a(US[:, :, 0:2], SCV[:, i, :, :], AF.Identity, rd=[SCV], wr=[US])
                k.tt(US[:, :, 2:10], v3(CT[:, :n]), v3(xa), ALU.mult, rd=[CT, px, US], wr=[US])
                z3 = v3(ZT[:, :n])
                k.ts(z3, US[:, :, 0:8], cw(i, 0), None, ALU.mult, None, rd=[US, VEC], wr=[ZT])
                k.stt(z3, US[:, :, 1:9], cw(i, 1), z3, ALU.mult, ALU.add, rd=[US, VEC, ZT], wr=[ZT])
                k.stt(z3, US[:, :, 2:10], cw(i, 2), z3, ALU.mult, ALU.add, rd=[US, VEC, ZT], wr=[ZT])
                k.tt(BZ[:, i, t0:t0 + n], ZT[:, :n], ba, ALU.mult, rd=[ZT, pb], wr=[BZ])
                k.a(CSO[:, i, :, :], US[:, :, 8:10], AF.Identity, rd=[US], wr=[CSO])
        gemm(conv_w_in[0], D, [[i * 128, D + i * 128, 2 * D + i * 128] for i in range(KC)],
             [(Hh, lambda kd, t0, n: Hh[:, kd, t0:t0 + n])], ev)
        k.dma(k.sp, o_convs[p], CSO[:], rd=[CSO])
        if p == NPASS - 1:
            k.dma(k.sp, o_convp, UC[:], rd=[UC])

        def ev_out(gi, ti, outs):
            t0, n = tiles[ti]
            k.a(Y[:, gi, t0:t0 + n], outs[0][1], AF.Identity, rd=[outs[0][0]], wr=[Y])
        gemm(conv_w_out[0], D, [[i * 128] for i in range(KC)],
             [(BZ, lambda kd, t0, n: BZ[:, kd, t0:t0 + n])], ev_out)
        postnorm(l, s, p, Y, RSTD, SQ, TMP)
        k.barrier()

    def rwkv(p):
        l, s = 1, 1
        cv = Carver()
        HX = cv.get([128, 2 * KC, TP], BF16)
        Y = T(HX.ap.rearrange("p a b -> p (a b)").bitcast(F32)[:, :KC * TP].rearrange("p (a b) -> p a b", a=KC))
        YG = cv.get([128, KC, TP], BF16)
        LW = cv.get([128, TP], BF16); LA = cv.get([128, TP], BF16); LG = cv.get([128, 2, TP], BF16)
        RSTD = cv.get([128, TP]); SQ = [cv.get([128, 512], BF16) for _ in range(2)]
        T1 = cv.get([128, TP]); T2 = cv.get([128, TP]); TMP = [T1, T2]
        HT = cv.get([128, 1 + TPR]); HSM = cv.get([128, NSS, 9])
        SSH = cv.get([128, KC, NSS]); SHO = cv.get([128, KC, NSS])
        SWB = [cv.get([64, NSS, 2, 64]) for _ in range(2)]
        k.dma(k.sp, SSH[:], sshift[p], wr=[SSH])

        def sink(kd, ti, t0, n, tmp):
            if ti == 0:
                k.a(HT[:, 0:1], HC[:, kd:kd + 1], AF.Identity, rd=[HC], wr=[HT])
                k.a(HT[:, 1:1 + n], tmp[:, :n], AF.Identity, rd=[tmp, HT], wr=[HT])
                k.a(HX[:, kd, t0:t0 + n], tmp[:, :n], AF.Identity, rd=[tmp], wr=[HX])
                k.tt(HX[:, KC + kd, t0:t0 + n], HT[:, 0:n], HT[:, 1:n + 1], ALU.subtract, rd=[HT], wr=[HX])
                k.a(HC[:, kd:kd + 1], HT[:, n:n + 1], AF.Identity, rd=[HT], wr=[HC])
            else:
                v3 = lambda ap: ap.rearrange("p (a b) -> p a b", a=NSS)
                k.a(HSM[:, :, 0:1], SSH[:, kd, :].unsqueeze(2), AF.Identity, rd=[SSH], wr=[HSM])
                k.a(HSM[:, :, 1:9], v3(tmp[:, :n]), AF.Identity, rd=[tmp, HSM], wr=[HSM])
                k.a(HX[:, kd, t0:t0 + n], tmp[:, :n], AF.Identity, rd=[tmp], wr=[HX])
                k.tt(v3(HX[:, KC + kd, t0:t0 + n]), HSM[:, :, 0:8], HSM[:, :, 1:9], ALU.subtract, rd=[HSM], wr=[HX])
                k.a(SHO[:, kd, :].unsqueeze(2), HSM[:, :, 8:9], AF.Identity, rd=[HSM], wr=[SHO])
        prenorm(l, s, p, RSTD, SQ, TMP, sink)
        k.dma(k.sp, o_shifts[p], SHO[:], rd=[SHO])
        if p == NPASS - 1:
            k.dma(k.sp, o_shiftp, HC[:], rd=[HC])
        acts2 = [(HX, lambda kd, t0, n: HX[:, kd, t0:t0 + n]), (HX, lambda kd, t0, n: HX[:, KC + kd, t0:t0 + n])]

        def ev_w(gi, ti, outs):
            t0, n = tiles[ti]
            k.a(LW[:96, t0:t0 + n], outs[0][1], AF.Tanh, rd=[outs[0][0]], wr=[LW])
        gemm(rw_w1[0], D, [[0]], acts2, ev_w, mix=V_MIX + 1, ncols=96)

        def ev_a(gi, ti, outs):
            t0, n = tiles[ti]
            k.a(LA[:96, t0:t0 + n], outs[0][1], AF.Identity, rd=[outs[0][0]], wr=[LA])
        gemm(rw_a1[0], D, [[0]], acts2, ev_a, mix=V_MIX + 4, ncols=96)

        def ev_g(gi, ti, outs):
            t0, n = tiles[ti]
            k.a(LG[:, gi, t0:t0 + n], outs[0][1], AF.Sigmoid, rd=[outs[0][0]], wr=[LG])
        gemm(rw_g1[0], D, [[0], [128]], acts2, ev_g, mix=V_MIX + 5)

        Fn = lambda: cv.get([128, TP])
        R = Fn(); KR = Fn(); V = Fn(); WS = Fn(); A = Fn(); G = Fn()
        KKN = Fn(); KM = Fn(); BP = KR; BON = A; EP = Fn(); YF = R
        Bn = lambda w=1: cv.get([128, w * TP], BF16)
        ART = Bn(2); KT = Bn(); BT = Bn(); KPT = Bn(); BPT = Bn(); VB = Bn()
        NCH = 4
        TOK = [cv.get([64, 5, 128], BF16) for _ in range(NCH)]
        AM = [cv.get([64, 4, 128], BF16) for _ in range(NCH)]
        INV = [[cv.get([64, 2, 3, 64], BF16) for _ in range(2)] for _ in range(NCH)]
        YTK = cv.get([64, 128]); HB = cv.get([64, 2, 64], BF16); GSH = cv.get([64, 2, 8])
        TAB = [cv.get([64, 2, 64], BF16) for _ in range(NCH)]; AVB = [cv.get([64, 2, 64], BF16) for _ in range(NCH)]
        TAVB = [cv.get([64, 2, 64], BF16) for _ in range(NCH)]; MTB = [cv.get([64, 2, 64], BF16) for _ in range(NCH)]
        GTB = [cv.get([64, 2, 64], BF16) for _ in range(NCH)]
        MASKA = {64: CST[:64, 2, :], 8: CST[:8, 3, 0:16]}
        MASKL = {64: CST[:64, 4, 0:64], 8: CST[:8, 4, 64:72]}
        RM = {64: CST[:, 2, :], 8: CST[:, 3, :]}

        def store_ev(dst):
            def ev(gi, ti, outs, dst=dst):
                t0, n = tiles[ti]
                k.a(dst[:, t0:t0 + n], outs[0][1], AF.Identity, rd=[outs[0][0]], wr=[dst])
            return ev

        if K_RW < 11:
            k.op(k.dve, lambda: nc.vector.memset(YG[:], 0.0), wr=[YG])
        for i in range(KC if K_RW >= 2 else 0):
            SW = SWB[i % 2]
            k.dma(k.sp, SW[:], swkv[p, i], wr=[SW])
            gemm(rw_wr[0], D, [[i * 128]], acts2, store_ev(R), mix=V_MIX + 0)
            gemm(rw_wk[0], D, [[i * 128]], acts2, store_ev(KR), mix=V_MIX + 2)
            gemm(rw_wv[0], D, [[i * 128]], acts2, store_ev(V), mix=V_MIX + 3)

            def ev_ws(gi, ti, outs, i=i):
                t0, n = tiles[ti]
                k.a(WS[:, t0:t0 + n], outs[0][1], AF.Sigmoid, rd=[outs[0][0], VEC], wr=[WS], bias=vcol(V_W0, i))
            gemm(rw_w2[0], 96, [[i * 128]], [(LW, lambda kd, t0, n: LW[:96, t0:t0 + n])], ev_ws)

            def ev_as(gi, ti, outs, i=i):
                t0, n = tiles[ti]
                k.a(A[:, t0:t0 + n], outs[0][1], AF.Sigmoid, rd=[outs[0][0], VEC], wr=[A], bias=vcol(V_A0, i))
            gemm(rw_a2[0], 96, [[i * 128]], [(LA, lambda kd, t0, n: LA[:96, t0:t0 + n])], ev_as)
            gemm(rw_g2[0], 256, [[i * 128]], [(LG, lambda kd, t0, n: LG[:, kd, t0:t0 + n])], store_ev(G))

            for ti, (t0, n) in enumerate(tiles):
                C = 64 if ti == 0 else 8
                nch = n // C
                sl = slice(t0, t0 + n)
                v3 = lambda ap, C=C: ap.rearrange("p (a b) -> p a b", b=C)
                k.ts(T1[:, sl], KR[:, sl], vcol(V_KK, i), None, ALU.mult, None, rd=[KR, VEC], wr=[T1])
                k.tt(T2[:, sl], T1[:, sl], T1[:, sl], ALU.mult, rd=[T1], wr=[T2])
                pst = k.ps()
                k.mm(pst, pst.ap[:, :n], [(blk, T2[:, sl])], rd=[CST, T2])
                k.ts(T2[:, sl], pst.ap[:, :n], 1e-24, None, ALU.max, None, rd=[pst], wr=[T2])
                k.rsqrt(T2, T2[:, sl], T2[:, sl], 1.0, 0.0, rd=[T2])
                k.tt(KKN[:, sl], T1[:, sl], T2[:, sl], ALU.mult, rd=[T1, T2], wr=[KKN])
                k.ts(T1[:, sl], A[:, sl], vcol(V_KA, i), vcol(V_OMKA, i), ALU.mult, ALU.add, rd=[A, VEC], wr=[T1])
                k.tt(KM[:, sl], KR[:, sl], T1[:, sl], ALU.mult, rd=[KR, T1], wr=[KM])
                k.tt(BP[:, sl], KKN[:, sl], A[:, sl], ALU.mult, rd=[KKN, A], wr=[BP])
                k.tt(T1[:, sl], R[:, sl], KM[:, sl], ALU.mult, rd=[R, KM], wr=[T1])
                k.ts(T1[:, sl], T1[:, sl], vcol(V_RK, i), None, ALU.mult, None, rd=[T1, VEC], wr=[T1])
                pst = k.ps()
                k.mm(pst, pst.ap[:, :n], [(blk, T1[:, sl])], rd=[CST, T1])
                k.tt(BON[:, sl], pst.ap[:, :n], V[:, sl], ALU.mult, rd=[pst, V], wr=[BON])
                k.ts(WS[:, sl], WS[:, sl], -0.6065306597126334, None, ALU.mult, None, rd=[WS], wr=[WS])
                rm = CST[:, 5, :] if False else None
                for c in range(nch):
                    cs = slice(t0 + c * C, t0 + (c + 1) * C)
                    k.op(k.dve, lambda cs=cs: nc.vector.tensor_tensor_scan(
                        T1[:, cs], CST[:, 5, :C], WS[:, cs], 0.0, ALU.mult, ALU.add), rd=[WS, CST], wr=[T1])
                CUM = T1
                k.a(EP[:, sl], CUM[:, sl], AF.Exp, rd=[CUM], wr=[EP])
                a3 = ART[:, 2 * t0:2 * (t0 + n)].rearrange("p (a t b) -> p a t b", t=2, b=C)
                k.tt(a3[:, :, 1, :], v3(R[:, sl]), v3(EP[:, sl]), ALU.mult, rd=[R, EP], wr=[ART])
                k.tt(T2[:, sl], CUM[:, sl], WS[:, sl], ALU.subtract, rd=[CUM, WS], wr=[T2])
                k.a(T2[:, sl], T2[:, sl], AF.Exp, rd=[T2], wr=[T2])
                k.stt(a3[:, :, 0, :], v3(KKN[:, sl]), -1.0, v3(T2[:, sl]), ALU.mult, ALU.mult, rd=[KKN, T2], wr=[ART])
                k.a(T2[:, sl], CUM[:, sl], AF.Exp, rd=[CUM], wr=[T2], scale=-1.0)
                k.tt(KT[:, sl], KM[:, sl], T2[:, sl], ALU.mult, rd=[KM, T2], wr=[KT])
                k.tt(BT[:, sl], BP[:, sl], T2[:, sl], ALU.mult, rd=[BP, T2], wr=[BT])
                c3 = v3(CUM[:, sl])
                k.tt(v3(T2[:, sl]), c3[:, :, C - 1:C].broadcast_to([128, nch, C]), c3, ALU.subtract, rd=[CUM], wr=[T2])
                k.a(T2[:, sl], T2[:, sl], AF.Exp, rd=[T2], wr=[T2])
                k.tt(KPT[:, sl], KM[:, sl], T2[:, sl], ALU.mult, rd=[KM, T2], wr=[KPT])
                k.tt(BPT[:, sl], BP[:, sl], T2[:, sl], ALU.mult, rd=[BP, T2], wr=[BPT])
                k.a(VB[:, sl], V[:, sl], AF.Identity, rd=[V], wr=[VB])
                ends = EP[:, t0 + C - 1:t0 + n:C]
                k.a(GSH[:, 0, :nch], EP[0:64, t0 + C - 1:t0 + n:C], AF.Identity, rd=[EP], wr=[GSH])
                pst = k.ps()
                k.mm(pst, pst.ap[:64, :nch], [(CST[:, 0, 64:128], ends)], rd=[CST, EP])
                k.a(GSH[:, 1, :nch], pst.ap[:64, :nch], AF.Identity, rd=[pst, GSH], wr=[GSH])

                all_units = list(range(nch))
                ftr = lambda ap, c: ap[:, t0 + c * C:t0 + (c + 1) * C]
                art = lambda c, w, h: ART[64 * h:64 * h + 64, 2 * t0 + c * 2 * C + w * C:2 * t0 + c * 2 * C + (w + 1) * C]
                art2 = lambda c, h: ART[64 * h:64 * h + 64, 2 * t0 + c * 2 * C:2 * t0 + (c + 1) * 2 * C]
                for u0 in range(0, nch if K_RW >= 3 else 0, NCH):
                  units = all_units[u0:u0 + NCH]
                  for c in units:
                      pst = k.ps()
                      pb = pst.ap.bitcast(BF16)[:C, :640].rearrange("p (a b) -> p a b", a=5)
                      for w, src in enumerate((VB, KPT, BPT)):
                          k.tr(pst, pb[:, w, :], ftr(src, c), IDB[:], rd=[src, IDB])
                      for w in range(2):
                          k.tr(pst, pb[:, 3 + w, :], ART[:, 2 * t0 + c * 2 * C + w * C:2 * t0 + c * 2 * C + (w + 1) * C],
                               IDB[:], rd=[ART, IDB])
                      k.a(TOK[c % NCH][:C, :, :], pb, AF.Identity, rd=[pst], wr=[TOK[c % NCH]])
                  if K_RW < 4:
                      continue
                  for c in units:
                      I0 = INV[c % NCH][0]
                      for h in range(2):
                          pst = k.ps()
                          pa = pst.ap[:C, :256].rearrange("p (a b) -> p a b", a=2)
                          for w, src in enumerate((KT, BT)):
                              k.mm(pst, pa[:, w, :2 * C], [(src[64 * h:64 * h + 64, t0 + c * C:t0 + (c + 1) * C],
                                                            art2(c, h))], rd=[src, ART])
                          k.tt(AM[c % NCH][:C, 2 * h:2 * h + 2, :2 * C], pa[:, :, :2 * C],
                               MASKA[C].unsqueeze(1).broadcast_to([C, 2, 2 * C]), ALU.mult, rd=[pst, CST], wr=[AM[c % NCH]])
                          pst2 = k.ps()
                          k.mm(pst2, pst2.ap[:C, :C], [(art(c, 0, h), BT[64 * h:64 * h + 64, t0 + c * C:t0 + (c + 1) * C])],
                               rd=[ART, BT])
                          k.tt(I0[:C, h, 2, :C], pst2.ap[:C, :C], MASKL[C], ALU.mult, rd=[pst2, CST], wr=[I0])
                      k.a(I0[:C, :, 0, :C], AM[c % NCH][:C, 1::2, 0:C], AF.Identity, rd=[AM[c % NCH], I0], wr=[I0])
                      k.a(I0[:C, :, 1, :C], IDB[:C, :C].unsqueeze(1).broadcast_to([C, 2, C]), AF.Identity,
                          rd=[IDB, I0], wr=[I0])
                  if K_RW < 5:
                      continue
                  nlev = 6 if C == 64 else 3
                  for lev in range(nlev):
                      last = lev == nlev - 1
                      for c in units:
                          Ia, Ib = INV[c % NCH][lev % 2], INV[c % NCH][(lev + 1) % 2]
                          pst = k.ps()
                          pi = pst.ap[:C, :384].rearrange("p (h w b) -> p h w b", h=2, w=3)
                          for h in range(2):
                              for w in range(2):
                                  k.mm(pst, pi[:, h, w, :C], [(Ia[:C, h, 2, :C], Ia[:C, h, w, :C])], rd=[Ia])
                              if not last:
                                  k.mm(pst, pi[:, h, 2, :C], [(Ia[:C, h, 0, :C], Ia[:C, h, 2, :C])], rd=[Ia])
                          if not last:
                              k.a(Ib[:C, :, 0, :C], pi[:, :, 0, :C], AF.Identity, rd=[pst], wr=[Ib])
                              k.a(Ib[:C, :, 2, :C], pi[:, :, 2, :C], AF.Identity, rd=[pst, Ib], wr=[Ib])
                          k.tt(Ib[:C, :, 1, :C], pi[:, :, 1, :C], Ia[:C, :, 1, :C], ALU.add, rd=[pst, Ia, Ib], wr=[Ib])
                  XI = nlev % 2
                  if K_RW < 6:
                      continue
                  for c in units:
                      u = c % NCH
                      X = INV[u][XI]
                      p1 = k.ps(); p1v = p1.ap[:C, :128].rearrange("p (a b) -> p a b", a=2)
                      p2 = k.ps(); p2v = p2.ap[:C, :128].rearrange("p (a b) -> p a b", a=2)
                      for h in range(2):
                          k.mm(p1, p1v[:, h, :], [(X[:C, h, 1, :C], TOK[u][:C, 3, 64 * h:64 * h + 64])], rd=[X, TOK[u]])
                          k.mm(p2, p2v[:, h, :], [(AM[u][:C, 2 * h, 0:C], TOK[u][:C, 0, 64 * h:64 * h + 64])],
                               rd=[AM[u], TOK[u]])
                      k.a(TAB[u][:C], p1v, AF.Identity, rd=[p1], wr=[TAB[u]])
                      k.cp(AVB[u][:C], p2v, rd=[p2], wr=[AVB[u]])
                  for c in units:
                      u = c % NCH
                      X = INV[u][XI]
                      p1 = k.ps(); p1v = p1.ap[:C, :128].rearrange("p (a b) -> p a b", a=2)
                      p2 = k.ps(); p2v = p2.ap[:64, :128].rearrange("p (a b) -> p a b", a=2)
                      p3 = k.ps(); p3v = p3.ap[:64, :2 * C].rearrange("p (a b) -> p a b", a=2)
                      for h in range(2):
                          k.mm(p1, p1v[:, h, :], [(X[:C, h, 1, :C], AVB[u][:C, h, :])], rd=[X, AVB[u]])
                          k.mm(p2, p2v[:, h, :], [(TAB[u][:C, h, :], TOK[u][:C, 2, 64 * h:64 * h + 64])],
                               rd=[TAB[u], TOK[u]])
                          k.mm(p3, p3v[:, h, :], [(TOK[u][:C, 4, 64 * h:64 * h + 64], IDB[:C, :C]),
                                                  (TAB[u][:C, h, :], AM[u][:C, 2 * h + 1, C:2 * C])],
                               rd=[TOK[u], IDB, TAB[u], AM[u]])
                      k.a(TAVB[u][:C], p1v, AF.Identity, rd=[p1], wr=[TAVB[u]])
                      k.cp(MTB[u][:], p2v, rd=[p2], wr=[MTB[u]])
                      k.a(GTB[u][:, :, :C], p3v, AF.Identity, rd=[p3], wr=[GTB[u]])
                  for c in units:
                      u = c % NCH
                      Hs = HS[:, i, :, :] if ti == 0 else SW[:, c, :, :]
                      Ht = HS if ti == 0 else SW
                      k.a(HB[:], Hs, AF.Identity, rd=[Ht], wr=[HB])
                      ph = k.ps(); phv = ph.ap[:64, :128].rearrange("p (a b) -> p a b", a=2)
                      py = k.ps(); pyv = py.ap[:C, :128].rearrange("p (a b) -> p a b", a=2)
                      for h in range(2):
                          hs_ = slice(64 * h, 64 * h + 64)
                          k.mm(ph, phv[:, h, :], [(TOK[u][:C, 1, hs_], TOK[u][:C, 0, hs_]),
                                                  (TOK[u][:C, 2, hs_], TAVB[u][:C, h, :]),
                                                  (MTB[u][:, h, :], HB[:, h, :])], rd=[TOK[u], TAVB[u], MTB[u], HB])
                      for h in range(2):
                          hs_ = slice(64 * h, 64 * h + 64)
                          k.mm(py, pyv[:, h, :], [(AM[u][:C, 2 * h, C:2 * C], TOK[u][:C, 0, hs_]),
                                                  (AM[u][:C, 2 * h + 1, C:2 * C], TAVB[u][:C, h, :]),
                                                  (GTB[u][:, h, :C], HB[:, h, :])], rd=[AM[u], TOK[u], TAVB[u], GTB[u], HB])
                      for h in range(2):
                          k.stt(Hs[:, h, :], Hs[:, h, :], GSH[:, h, c:c + 1], phv[:, h, :], ALU.mult, ALU.add,
                                rd=[Ht, GSH, ph], wr=[Ht])
                      k.a(YTK[:C, :], py.ap[:C, :128], AF.Identity, rd=[py], wr=[YTK])
                      pst = k.ps()
                      k.tr(pst, pst.ap[:, :C], YTK[:C, :], ident[:C, :C], rd=[YTK, CST])
                      k.cp(YF[:, t0 + c * C:t0 + (c + 1) * C], pst.ap[:, :C], rd=[pst], wr=[YF])
                if K_RW < 11:
                    continue
                pst = k.ps()
                k.mm(pst, pst.ap[:, :n], [(blk, YF[:, sl])], rd=[CST, YF])
                k.stt(T1[:, sl], pst.ap[:, :n], -1.0 / 64, YF[:, sl], ALU.mult, ALU.add, rd=[pst, YF], wr=[T1])
                k.tt(T2[:, sl], T1[:, sl], T1[:, sl], ALU.mult, rd=[T1], wr=[T2])
                pst = k.ps()
                k.mm(pst, pst.ap[:, :n], [(blk, T2[:, sl])], rd=[CST, T2])
                k.rsqrt(T2, T2[:, sl], pst.ap[:, :n], 1.0 / 64, GN_EPS, rd=[pst])
                k.tt(T1[:, sl], T1[:, sl], T2[:, sl], ALU.mult, rd=[T1, T2], wr=[T1])
                k.ts(T1[:, sl], T1[:, sl], vcol(V_LNW, i), vcol(V_LNB, i), ALU.mult, ALU.add, rd=[T1, VEC], wr=[T1])
                k.tt(T1[:, sl], T1[:, sl], BON[:, sl], ALU.add, rd=[T1, BON], wr=[T1])
                k.tt(YG[:, i, sl], T1[:, sl], G[:, sl], ALU.mult, rd=[T1, G], wr=[YG])
            k.dma(k.sp, o_wkvs[p, i], SW[:], rd=[SW])
        if p == NPASS - 1:
            k.dma(k.sp, o_wkvp, HS[:], rd=[HS])

        def ev_out(gi, ti, outs):
            t0, n = tiles[ti]
            k.a(Y[:, gi, t0:t0 + n], outs[0][1], AF.Identity, rd=[outs[0][0]], wr=[Y, HX])
        gemm(rw_wo[0], D, [[i * 128] for i in range(KC)],
             [(YG, lambda kd, t0, n: YG[:, kd, t0:t0 + n])], ev_out)
        postnorm(l, s, p, Y, RSTD, SQ, TMP)
        k.barrier()


    for p in range(NPASS):
        k.dma(k.sp, XT[:, :, 0:TPR], xp[p], wr=[XT])
        k.dma(k.sp, XT[:, :, TPR:TP], xs[p], wr=[XT])
        nsub = 0
        for l in range(2):
            for s in range(3):
                if nsub >= DBG_NSUB:
                    break
                if s == 0:
                    ffn(l, 0, 0, p)
                elif s == 2:
                    ffn(l, 1, 2, p)
                elif l == 0:
                    conv(p)
                else:
                    rwkv(p)
                nsub += 1
        k.dma(k.sp, yp[p], XT[:, :, 0:TPR], rd=[XT])
        k.dma(k.sp, ys[p], XT[:, :, TPR:TP], rd=[XT])
        k.barrier()
    k.barrier()
    return nc


_NC = None


def _fm(a):
    r = a.shape[0]
    return np.ascontiguousarray(a.reshape(r, KC, 128).transpose(2, 1, 0))


def _fm_inv(a):
    r = a.shape[2]
    return np.ascontiguousarray(a.transpose(2, 1, 0).reshape(r, D))


def _consts():
    c = np.zeros((128, 6, 128), np.float32)
    c[:, 0, :] = np.eye(128, dtype=np.float32)
    hh = np.arange(128) // 64
    c[:, 1, :] = (hh[:, None] == hh[None, :]).astype(np.float32)
    s = np.arange(64)[:, None]; t = np.arange(64)[None, :]
    c[:64, 2, 0:64] = (s < t); c[:64, 2, 64:128] = (s <= t)
    s8 = np.arange(8)[:, None]; t8 = np.arange(8)[None, :]
    c[:8, 3, 0:8] = (s8 < t8); c[:8, 3, 8:16] = (s8 <= t8)
    c[:64, 4, 0:64] = (s > t)
    c[:8, 4, 64:72] = (s8 > t8)
    c[:, 5, :] = 1.0
    return c


def kernel(**inp):
    global _NC
    f = lambda n: np.asarray(inp[n], dtype=np.float32)
    x_prompt, x_sample = f("x_prompt"), f("x_sample")
    state_conv, state_shift, state_wkv = f("state_conv"), f("state_shift"), f("state_wkv")
    c_prompt, c_sample = f("c_prompt"), f("c_sample")
    if _NC is None:
        _NC = build()
    nc = _NC
    vec = np.zeros((128, NVEC, KC), np.float32)
    put = lambda idx, v: vec.__setitem__((slice(None), idx), v.reshape(KC, 128).T)
    for l in range(2):
        for s in range(3):
            put(V_NPRE + l * 3 + s, f("norm_pre")[l, s]); put(V_NPOST + l * 3 + s, f("norm_post")[l, s])
    for j in range(3):
        put(V_CW + j, f("conv_w")[0, j])
    for j in range(6):
        put(V_MIX + j, f("rw_mix")[0, j])
    put(V_W0, f("rw_w0")[0]); put(V_A0, f("rw_a0")[0]); put(V_KK, f("rw_kk")[0]); put(V_KA, f("rw_ka")[0])
    put(V_RK, f("rw_rk")[0].reshape(-1)); put(V_LNW, f("rw_lnw")[0]); put(V_LNB, f("rw_lnb")[0])
    modb = np.ascontiguousarray(f("mod_b").reshape(2, 144, 128).transpose(2, 0, 1))
    cst = _consts()
    shared = {n: f(n) for n in ("mod_w", "ffn_w_in", "ffn_w_out", "conv_w_in", "conv_w_out", "rw_w1", "rw_w2",
                                 "rw_a1", "rw_a2", "rw_g1", "rw_g2", "rw_wr", "rw_wk", "rw_wv", "rw_wo")}
    in_maps = []
    for c in range(NCORES):
        sp_ = c % 4
        m = dict(shared)
        m["vec"] = vec; m["modb"] = modb; m["cst"] = cst
        m["xp"] = np.stack([_fm(x_prompt[sp_, p * TPR:(p + 1) * TPR]) for p in range(NPASS)])
        xs_c = x_sample[16 * c:16 * c + 16]
        m["xs"] = np.stack([_fm(xs_c[NSS * p:NSS * p + NSS].reshape(TSM, D)) for p in range(NPASS)])
        cc = np.concatenate([c_prompt[sp_:sp_ + 1], c_sample[16 * c:16 * c + 16]], 0)
        m["cT"] = _fm(cc)
        sc = state_conv[0, 16 * c:16 * c + 16]
        m["sconv"] = np.stack([_fm(sc[NSS * p:NSS * p + NSS].reshape(NSS * 2, D)).reshape(128, KC, NSS, 2)
                               for p in range(NPASS)])
        ss = state_shift[0, 16 * c:16 * c + 16]
        m["sshift"] = np.stack([_fm(ss[NSS * p:NSS * p + NSS]) for p in range(NPASS)])
        sw = state_wkv[0, 16 * c:16 * c + 16]
        swh = sw.reshape(NPASS, NSS, KC, 2, 64, 64).transpose(0, 2, 5, 1, 3, 4).reshape(NPASS, KC, 64, NSS, 2, 64)
        m["swkv"] = np.ascontiguousarray(swh)
        in_maps.append(m)
    res = run_bass_kernel_spmd(nc, in_maps, core_ids=list(range(NCORES)))
    R = res.results
    y_prompt = np.zeros((4, 2048, D), np.float32); y_sample = np.zeros((128, 8, D), np.float32)
    conv_p = np.zeros((1, 4, 2, D), np.float32); shift_p = np.zeros((1, 4, D), np.float32)
    wkv_p = np.zeros((1, 4, 32, 64, 64), np.float32)
    conv_s = np.zeros((1, 128, 2, D), np.float32); shift_s = np.zeros((1, 128, D), np.float32)
    wkv_s = np.zeros((1, 128, 32, 64, 64), np.float32)
    for c in range(NCORES):
        r = R[c]
        if c < 4:
            for p in range(NPASS):
                y_prompt[c, p * TPR:(p + 1) * TPR] = _fm_inv(r["yp"][p])
            conv_p[0, c] = _fm_inv(r["o_convp"])
            shift_p[0, c] = _fm_inv(r["o_shiftp"][:, :, None])[0]
            wkv_p[0, c] = r["o_wkvp"].transpose(1, 2, 3, 0).reshape(32, 64, 64)
        for p in range(NPASS):
            q = 16 * c + NSS * p
            y_sample[q:q + NSS] = _fm_inv(r["ys"][p]).reshape(NSS, 8, D)
            conv_s[0, q:q + NSS] = _fm_inv(r["o_convs"][p].reshape(128, KC, NSS * 2)).reshape(NSS, 2, D)
            shift_s[0, q:q + NSS] = _fm_inv(r["o_shifts"][p])
            wkv_s[0, q:q + NSS] = r["o_wkvs"][p].transpose(2, 0, 3, 4, 1).reshape(NSS, 32, 64, 64)
    return (y_prompt, y_sample, conv_p, shift_p, wkv_p, conv_s, shift_s, wkv_s)


if __name__ == "__main__":
    import time
    t0 = time.time()
    nc = build()
    print("built in", time.time() - t0)
```
